# Optimizing a Trainium2 kernel written in Bass

```python
import math
import jax, jax.numpy as jnp
from jax import lax
import numpy as np

D_MODEL = 1024
BATCH = 8
SEQ = 8192
DEPTH = 1

CHUNK = 64
Q_BLOCK = 128
POOL_WINDOWS = (2, 4, 8, 16)
POOL_WIDTH = D_MODEL // 2
POOL_GROUP = POOL_WIDTH // len(POOL_WINDOWS)
N_HEADS = D_MODEL // 128
HEAD_DIM = 64
V_DIM = 2 * HEAD_DIM
ATTN_QK_WIDTH = N_HEADS * 2 * HEAD_DIM
ATTN_V_WIDTH = N_HEADS * V_DIM
N_BRANCH = 2
IN_WIDTH = POOL_WIDTH + 2 * ATTN_QK_WIDTH + ATTN_V_WIDTH + N_BRANCH * D_MODEL
D_FF = 4 * D_MODEL
ROPE_THETA = 500000.0
ROPE_DIM = HEAD_DIM // 4
NORM_EPS = 1e-6
SUBLN_EPS = 1e-5

kernel_name = "hybrid_pool_diffattn_gated_block"


def rms_norm(x, g, eps=NORM_EPS):
    x32 = x.astype(jnp.float32)
    y = x32 * lax.rsqrt(jnp.mean(x32 * x32, axis=-1, keepdims=True) + eps)
    return (y * g.astype(jnp.float32)).astype(x.dtype)


def rope_tables(seq):
    pos = jnp.arange(seq, dtype=jnp.float32)
    inv = ROPE_THETA ** (-jnp.arange(0, ROPE_DIM, 2, dtype=jnp.float32) / ROPE_DIM)
    ang = pos[:, None] * inv[None, :]
    return jnp.cos(ang), jnp.sin(ang)


def partial_rope(t, cos, sin):
    half = ROPE_DIM // 2
    c = cos[None, :, None, None, :].astype(t.dtype)
    s = sin[None, :, None, None, :].astype(t.dtype)
    t1 = t[..., :half]
    t2 = t[..., half:ROPE_DIM]
    return jnp.concatenate([t1 * c - t2 * s, t2 * c + t1 * s, t[..., ROPE_DIM:]], axis=-1)


def multiscale_pool(u, w_group, scale):
    B, S, _ = u.shape
    ug = u.reshape(B, S, len(POOL_WINDOWS), POOL_GROUP)
    t = jnp.arange(1, S + 1, dtype=jnp.float32)
    outs = []
    for gi, w in enumerate(POOL_WINDOWS):
        xg = ug[:, :, gi, :].astype(jnp.float32)
        cs = jnp.cumsum(xg, axis=1)
        cs_prev = jnp.pad(cs, ((0, 0), (w, 0), (0, 0)))[:, :S]
        count = jnp.minimum(t, float(w))
        mean = (cs - cs_prev) / count[None, :, None]
        outs.append((mean - xg).astype(u.dtype))
    pooled = jnp.stack(outs, axis=2)
    mixed = jnp.einsum('bsgp,gpq->bsgq', pooled, w_group)
    return mixed.reshape(B, S, POOL_WIDTH) * scale


def diff_attention(q, k, v, lam):
    B, S = q.shape[0], q.shape[1]
    nb = S // Q_BLOCK
    qb = q.reshape(B, nb, Q_BLOCK, N_HEADS, 2, HEAD_DIM).transpose(1, 0, 2, 3, 4, 5)
    key_chunk = jnp.arange(S) // CHUNK
    scale = HEAD_DIM ** -0.5

    def one_block(args):
        i, qi = args
        q_chunk = (i * Q_BLOCK + jnp.arange(Q_BLOCK)) // CHUNK
        allowed = key_chunk[None, :] <= q_chunk[:, None]
        s = jnp.einsum('bqhcd,bkhcd->bhcqk', qi, k,
                       preferred_element_type=jnp.float32) * scale
        s = jnp.where(allowed, s, -jnp.inf)
        p = jax.nn.softmax(s, axis=-1)
        a = p[:, :, 0] - lam * p[:, :, 1]
        return jnp.einsum('bhqk,bkhe->bqhe', a.astype(v.dtype), v)

    out = lax.map(one_block, (jnp.arange(nb), qb))
    return out.transpose(1, 0, 2, 3, 4).reshape(B, S, N_HEADS, V_DIM)


def setup_inputs(seed: int = 0) -> dict:
    key = jax.random.key(seed)
    ks = jax.random.split(key, 20)
    f32 = jnp.float32
    nrm = lambda k, shp, s: jax.random.normal(k, shp, f32) * s
    L = DEPTH
    return {
        "x": nrm(ks[0], (BATCH, SEQ, D_MODEL), 1.0),
        "w_in": nrm(ks[1], (L, D_MODEL, IN_WIDTH), D_MODEL ** -0.5),
        "b_gate": nrm(ks[2], (L, N_BRANCH, D_MODEL), 0.1),
        "pool_w": nrm(ks[3], (L, len(POOL_WINDOWS), POOL_GROUP, POOL_GROUP), POOL_GROUP ** -0.5),
        "pool_scale": 1.0 + nrm(ks[4], (L, POOL_WIDTH), 0.1),
        "lambda_q1": nrm(ks[5], (L, HEAD_DIM), 0.1),
        "lambda_k1": nrm(ks[6], (L, HEAD_DIM), 0.1),
        "lambda_q2": nrm(ks[7], (L, HEAD_DIM), 0.1),
        "lambda_k2": nrm(ks[8], (L, HEAD_DIM), 0.1),
        "g_subln": 1.0 + nrm(ks[9], (L, V_DIM), 0.1),
        "w_pool_out": nrm(ks[10], (L, POOL_WIDTH, D_MODEL), POOL_WIDTH ** -0.5),
        "w_attn_out": nrm(ks[11], (L, ATTN_V_WIDTH, D_MODEL), ATTN_V_WIDTH ** -0.5),
        "w_o": nrm(ks[12], (L, D_MODEL, D_MODEL), D_MODEL ** -0.5),
        "g_mix": 1.0 + nrm(ks[13], (L, D_MODEL), 0.1),
        "g_mlp": 1.0 + nrm(ks[14], (L, D_MODEL), 0.1),
        "w_up": nrm(ks[15], (L, D_MODEL, D_FF), D_MODEL ** -0.5),
        "w_down": nrm(ks[16], (L, D_FF, D_MODEL), D_FF ** -0.5),
        "g_final": 1.0 + nrm(ks[17], (D_MODEL,), 0.1),
    }


def reference(x, w_in, b_gate, pool_w, pool_scale, lambda_q1, lambda_k1, lambda_q2, lambda_k2,
              g_subln, w_pool_out, w_attn_out, w_o, g_mix, g_mlp, w_up, w_down, g_final):
    B, S, _ = x.shape
    cos, sin = rope_tables(S)
    o1 = POOL_WIDTH
    o2 = o1 + ATTN_QK_WIDTH
    o3 = o2 + ATTN_QK_WIDTH
    o4 = o3 + ATTN_V_WIDTH
    for l in range(DEPTH):
        lambda_init = 0.8 - 0.6 * math.exp(-0.3 * l)
        h = rms_norm(x, g_mix[l])
        proj = h @ w_in[l]
        u_pool = proj[..., :o1]
        q = proj[..., o1:o2].reshape(B, S, N_HEADS, 2, HEAD_DIM)
        k = proj[..., o2:o3].reshape(B, S, N_HEADS, 2, HEAD_DIM)
        v = proj[..., o3:o4].reshape(B, S, N_HEADS, V_DIM)
        gates = jax.nn.sigmoid(proj[..., o4:].reshape(B, S, N_BRANCH, D_MODEL) + b_gate[l])
        y_pool = multiscale_pool(u_pool, pool_w[l], pool_scale[l]) @ w_pool_out[l]
        q = partial_rope(q, cos, sin)
        k = partial_rope(k, cos, sin)
        lam = (jnp.exp(jnp.sum(lambda_q1[l].astype(jnp.float32) * lambda_k1[l].astype(jnp.float32)))
               - jnp.exp(jnp.sum(lambda_q2[l].astype(jnp.float32) * lambda_k2[l].astype(jnp.float32)))
               + lambda_init)
        att = diff_attention(q, k, v, lam)
        att = rms_norm(att, g_subln[l], SUBLN_EPS) * (1.0 - lambda_init)
        y_attn = att.reshape(B, S, ATTN_V_WIDTH) @ w_attn_out[l]
        merged = gates[:, :, 0] * y_pool + gates[:, :, 1] * y_attn
        x = x + merged @ w_o[l]
        h2 = rms_norm(x, g_mlp[l])
        x = x + jnp.square(jax.nn.relu(h2 @ w_up[l])) @ w_down[l]
    return rms_norm(x, g_final)
```

```python
import contextlib
import numpy as np
import concourse.bass as bass
import concourse.mybir as mybir
from concourse.bass_utils import run_bass_kernel_spmd

F32 = mybir.dt.float32
BF16 = mybir.dt.bfloat16
ALU = mybir.AluOpType
AF = mybir.ActivationFunctionType
AX = mybir.AxisListType

D = 1024
INW = 5632
DFF = 4096
O_U, O_Q, O_K, O_V, O_GA, O_GB = 0, 512, 1536, 2560, 3584, 4608
NV = 45
SB_BASE = 17408
SB_LIMIT = 229000

COMPUTE = ("pe", "act", "dve", "pool")
SAME_ENG_RAW = ("act", "dve", "pool")


class Op:
    __slots__ = ("eng", "idx", "fn", "deps", "dma_deps", "sig", "key", "cum", "is_dma")

    def __init__(self, eng, idx, fn):
        self.eng = eng
        self.idx = idx
        self.fn = fn
        self.deps = {}
        self.dma_deps = {}
        self.sig = None
        self.key = None
        self.cum = None
        self.is_dma = False


class Prog:
    def __init__(self, nc):
        self.nc = nc
        self.ops = {e: [] for e in ("pe", "act", "dve", "pool", "sp")}
        self.last_w = {}
        self.readers = {}
        self.dma_cum = {}
        self.dma_keys = []

    def _add_dep(self, op, src):
        if src is None or src is op:
            return
        if src.is_dma:
            op.dma_deps[src.key] = max(op.dma_deps.get(src.key, 0), self.dma_cum[src.key])
        else:
            op.deps[src.eng] = max(op.deps.get(src.eng, -1), src.idx)

    def op(self, eng, fn, reads=(), writes=(), dma_key=None):
        lst = self.ops[eng]
        o = Op(eng, len(lst), fn)
        if dma_key is not None:
            o.is_dma = True
            o.key = dma_key
        def same(src):
            return (not src.is_dma) and src.eng == eng and not o.is_dma

        for r in reads:
            src = self.last_w.get(r)
            if src is None:
                continue
            if same(src):
                if eng in SAME_ENG_RAW:
                    o.deps[eng] = max(o.deps.get(eng, -1), src.idx)
            else:
                self._add_dep(o, src)
        for w in writes:
            src = self.last_w.get(w)
            if src is not None:
                if same(src):
                    if eng in SAME_ENG_RAW:
                        o.deps[eng] = max(o.deps.get(eng, -1), src.idx)
                else:
                    self._add_dep(o, src)
            for rd in self.readers.get(w, ()):
                if same(rd):
                    if eng in SAME_ENG_RAW:
                        o.deps[eng] = max(o.deps.get(eng, -1), rd.idx)
                else:
                    self._add_dep(o, rd)
        if o.is_dma:
            if dma_key not in self.dma_cum:
                self.dma_cum[dma_key] = 0
                self.dma_keys.append(dma_key)
            self.dma_cum[dma_key] += 16
            o.cum = self.dma_cum[dma_key]
        for w in writes:
            self.last_w[w] = o
            self.readers[w] = []
        for r in reads:
            self.readers.setdefault(r, []).append(o)
        lst.append(o)
        return o

    def barrier(self):
        for e in self.ops:
            o = Op(e, len(self.ops[e]), None)
            for e2 in COMPUTE:
                if e2 == e:
                    continue
                j = len(self.ops[e2]) - 1
                while j >= 0 and (self.ops[e2][j].is_dma or self.ops[e2][j].fn is None):
                    j -= 1
                if j >= 0:
                    o.deps[e2] = j
            for k, c in self.dma_cum.items():
                o.dma_deps[k] = c
            self.ops[e].append(o)
        self.last_w = {}
        self.readers = {}

    def emit(self, final_wait_keys=()):
        nc = self.nc
        need = {e: set() for e in self.ops}
        for e, lst in self.ops.items():
            for o in lst:
                for e2, idx in o.deps.items():
                    need[e2].add(idx)
        sigmap = {}
        for e, lst in self.ops.items():
            n = 0
            arr = []
            for o in lst:
                if (not o.is_dma) and o.fn is not None and o.idx in need[e]:
                    n += 1
                    o.sig = n
                arr.append(n)
            sigmap[e] = arr
        _DBG["stats"] = ({e: (sigmap[e][-1] if sigmap[e] else 0) for e in sigmap}, {e: len(self.ops[e]) for e in self.ops}, max(self.dma_cum.values()), len(self.dma_keys))
        with contextlib.ExitStack() as st:
            sems = {e: st.enter_context(nc.semaphore("s_" + e)) for e in COMPUTE}
            dsem = {}
            for i, k in enumerate(self.dma_keys):
                dsem[k] = st.enter_context(nc.semaphore("d%d" % i))
            block = st.enter_context(nc.Block())
            engobj = {"pe": block.tensor, "act": block.scalar, "dve": block.vector,
                      "pool": block.gpsimd, "sp": block.sync}
            for e in ("sp", "pool", "act", "dve", "pe"):
                lst = self.ops[e]

                def body(eng, e=e, lst=lst):
                    known = {}
                    kdma = {}
                    for o in lst:
                        for e2, idx in o.deps.items():
                            v = sigmap[e2][idx]
                            if known.get(e2, 0) < v:
                                eng.wait_ge(sems[e2], v)
                                known[e2] = v
                        for k, c in o.dma_deps.items():
                            if kdma.get(k, 0) < c:
                                eng.wait_ge(dsem[k], c)
                                kdma[k] = c
                        if o.fn is None:
                            continue
                        ins = o.fn(eng)
                        if o.is_dma:
                            ins.then_inc(dsem[o.key], 16)
                        elif o.sig is not None:
                            ins.then_inc(sems[e], 1)
                    if e == "sp":
                        for k in final_wait_keys:
                            eng.wait_ge(dsem[k], self.dma_cum[k])

                engobj[e](body)


_UID = [0]
_DBG = {}


class Arena:
    def __init__(self, nc, base):
        self.nc = nc
        self.cur = base

    def alloc(self, shape, dtype):
        n = 1
        for s in shape[1:]:
            n *= s
        nbytes = n * (4 if dtype == F32 else 2)
        off = self.cur
        self.cur += (nbytes + 63) // 64 * 64
        assert self.cur <= SB_LIMIT, ("SBUF overflow", self.cur)
        _UID[0] += 1
        return self.nc.alloc_sbuf_tensor_at("t%d" % _UID[0], list(shape), dtype, offset=off)


class Rot:
    def __init__(self, n):
        self.n = n
        self.i = -1

    def next(self):
        self.i = (self.i + 1) % self.n
        return self.i


def build_program(S, dbg=False, phases="AB12"):
    nc = bass.Bass("TRN2", target_bir_lowering=False)
    P = Prog(nc)
    skind = "ExternalOutput" if dbg else "Internal"

    def din(name, shape, dt=F32):
        return nc.dram_tensor(name, list(shape), dt, kind="ExternalInput").ap()

    x_d = din("x", [S, D])
    w_in_d = din("w_in", [D, INW])
    pool_w_d = din("pool_w", [4, 128, 128])
    w_po_d = din("w_pool_out", [512, D])
    w_ao_d = din("w_attn_out", [D, D])
    w_o_d = din("w_o", [D, D])
    w_up_d = din("w_up", [D, DFF])
    w_dn_d = din("w_down", [DFF, D])
    vecs_d = din("vecs", [128, NV])
    lams_d = din("lams", [128, 256])
    ident_d = din("ident", [128, 128])
    cos_d = din("cosT", [128, S])
    sin_d = din("sinT", [128, S])
    invc_d = din("invc", [128, 64])
    out_d = nc.dram_tensor("out", [S, D], F32, kind="ExternalOutput").ap()

    QT_d = nc.dram_tensor("QT_s", [D, S], BF16, kind=skind).ap()
    KT_d = nc.dram_tensor("KT_s", [D, S], BF16, kind=skind).ap()
    V_d = nc.dram_tensor("V_s", [S, D], BF16, kind=skind).ap()
    mA_d = nc.dram_tensor("mA_s", [D, S], F32, kind=skind).ap()
    gB_d = nc.dram_tensor("gB_s", [D, S], F32, kind=skind).ap()
    aT_d = nc.dram_tensor("attT_s", [D, S], BF16, kind=skind).ap()

    def res_view(i, T):
        return out_d[i * T:(i + 1) * T, :].rearrange("a (b t) -> (a b) t", t=T)

    PSh = nc.alloc_psum_tensor("PS", [128, 8, 512], F32)
    PS = PSh
    PSb = PSh.bitcast(BF16)
    bank = Rot(8)

    def dma(q, out, in_, reads, writes, key):
        P.op(q, lambda e: e.dma_start(out=out, in_=in_), reads=reads, writes=writes, dma_key=key)

    def mm(out, lhsT, rhs, start, stop, reads, writes, skip=False):
        P.op("pe", lambda e: e.matmul(out, lhsT=lhsT, rhs=rhs, start=start, stop=stop, skip_group_check=skip),
             reads=reads, writes=writes)

    def tr(out, in_, idn, reads, writes):
        P.op("pe", lambda e: e.transpose(out=out, in_=in_, identity=idn), reads=reads, writes=writes)

    def act(out, in_, func, reads, writes, bias=None, scale=1.0):
        if bias is None:
            P.op("act", lambda e: e.activation(out=out, in_=in_, func=func, scale=scale), reads=reads, writes=writes)
        else:
            P.op("act", lambda e: e.activation(out=out, in_=in_, func=func, bias=bias, scale=scale),
                 reads=reads, writes=writes)

    def cp(eng, out, in_, reads, writes):
        if eng == "act":
            P.op("act", lambda e: e.copy(out=out, in_=in_), reads=reads, writes=writes)
        else:
            P.op(eng, lambda e: e.tensor_copy(out=out, in_=in_), reads=reads, writes=writes)

    def tt(eng, out, in0, in1, op, reads, writes):
        P.op(eng, lambda e: e.tensor_tensor(out=out, in0=in0, in1=in1, op=op), reads=reads, writes=writes)

    def ts(eng, out, in0, s1, s2, op0, op1, reads, writes):
        if s2 is None:
            P.op(eng, lambda e: e.tensor_scalar(out=out, in0=in0, scalar1=s1, scalar2=None, op0=op0),
                 reads=reads, writes=writes)
        else:
            P.op(eng, lambda e: e.tensor_scalar(out=out, in0=in0, scalar1=s1, scalar2=s2, op0=op0, op1=op1),
                 reads=reads, writes=writes)

    def stt(eng, out, in0, scalar, in1, op0, op1, reads, writes):
        P.op(eng, lambda e: e.scalar_tensor_tensor(out=out, in0=in0, scalar=scalar, in1=in1, op0=op0, op1=op1),
             reads=reads, writes=writes)

    def memset(eng, ap, val, writes):
        P.op(eng, lambda e: e.memset(ap, val), writes=writes)

    def recip(out, in_, reads, writes):
        P.op("dve", lambda e: e.reciprocal(out=out, in_=in_), reads=reads, writes=writes)

    def rsum(out, in_, reads, writes):
        P.op("dve", lambda e: e.tensor_reduce(out=out, in_=in_, axis=AX.X, op=ALU.add), reads=reads, writes=writes)

    A0 = Arena(nc, SB_BASE)
    vec = A0.alloc([128, NV], F32)
    ident = A0.alloc([128, 128], F32)
    identb = A0.alloc([128, 128], BF16)
    onesb = A0.alloc([128, 128], BF16)
    lam_t = A0.alloc([128, 256], F32)
    lam_p = A0.alloc([128, 64], F32)
    lam_s = A0.alloc([128, 4], F32)
    neglam = A0.alloc([128, 1], F32)
    eps6 = A0.alloc([128, 1], F32)
    eps5 = A0.alloc([128, 1], F32)
    invc = A0.alloc([128, 64], F32)
    PH_BASE = A0.cur

    dma("sp", vec[:], vecs_d, [], ["vec"], "c_vec")
    dma("sp", ident[:], ident_d, [], ["ident"], "c_ident")
    dma("sp", lam_t[:], lams_d, [], ["lam_t"], "c_lam")
    dma("sp", invc[:], invc_d, [], ["invc"], "c_invc")
    memset("dve", onesb[:], 1.0 / 1024.0, ["onesb"])
    memset("dve", eps6[:], 1e-6, ["eps6"])
    memset("dve", eps5[:], 1e-5, ["eps5"])
    cp("dve", identb[:], ident[:], ["ident"], ["identb"])
    for k in range(2):
        tt("dve", lam_p[:], lam_t[:, 128 * k:128 * k + 64], lam_t[:, 128 * k + 64:128 * k + 128], ALU.mult,
           ["lam_t"], ["lam_p"])
        rsum(lam_s[:, k:k + 1], lam_p[:], ["lam_p"], ["lam_s"])
    act(lam_s[:, 2:4], lam_s[:, 0:2], AF.Exp, ["lam_s"], ["lam_e"])
    tt("dve", neglam[:], lam_s[:, 3:4], lam_s[:, 2:3], ALU.subtract, ["lam_e"], ["neglam"])
    ts("dve", neglam[:], neglam[:], -0.2, None, ALU.add, None, ["neglam"], ["neglam"])

    cast_eng = Rot(3)
    CAST_ENGS = ("dve", "pool", "act")

    def load_cast(stage, srot, dst, src, ncols, perm=False, src_view=None):
        sl = srot.next()
        st_ap = stage[sl][:, 0:ncols] if src_view is None else src_view(stage[sl])
        dma("sp", st_ap, src, [], [("stage", sl)], ("stage", sl))
        if perm:
            eng = ("dve", "pool")[cast_eng.next() % 2]
            iv = stage[sl][:, 0:1024].rearrange("p (hc j dd) -> p j hc dd", hc=16, j=8, dd=8)
            ov = dst.rearrange("p (j hc dd) -> p j hc dd", j=8, hc=16, dd=8)
            _UID[0] += 1
            cp(eng, ov, iv, [("stage", sl)], [("W", _UID[0])])
        else:
            eng = CAST_ENGS[cast_eng.next()]
            _UID[0] += 1
            cp(eng, dst, stage[sl][:, 0:ncols], [("stage", sl)], [("W", _UID[0])])

    if "A" in phases:
        T = 256
        NT = S // T
        L = 16 + T
        A = Arena(nc, PH_BASE)
        Wb = A.alloc([128, 8, INW], BF16)
        Wpo = A.alloc([128, 4, D], BF16)
        Wpw = A.alloc([128, 4, 128], BF16)
        mark = A.cur
        stage = [A.alloc([128, 1024], F32) for _ in range(6)]
        srot = Rot(6)
        for dc in range(8):
            rows = w_in_d[dc * 128:(dc + 1) * 128, :]
            for (o, n, perm) in ((O_U, 512, False), (O_Q, 1024, True), (O_K, 1024, True),
                                 (O_V, 1024, False), (O_GA, 1024, False), (O_GB, 1024, False)):
                load_cast(stage, srot, Wb[:, dc, o:o + n], rows[:, o:o + n], n, perm)
        for g in range(4):
            load_cast(stage, srot, Wpo[:, g, :], w_po_d[g * 128:(g + 1) * 128, :], 1024)
        load_cast(stage, srot, Wpw[:].rearrange("p g q -> p (g q)"), pool_w_d.rearrange("g p q -> p g q"), 512,
                  src_view=lambda s: s[:, 0:512].rearrange("p (g q) -> p g q", g=4))
        P.barrier()
        A.cur = mark
        xin = [A.alloc([128, 2, D], F32) for _ in range(2)]
        xT2 = [A.alloc([128, 8, T], F32) for _ in range(2)]
        sq2 = [A.alloc([128, 8, T], BF16) for _ in range(2)]
        R1 = A.alloc([128, 8, T], BF16)
        rs2 = [A.alloc([128, T], F32) for _ in range(2)]
        hT2 = [A.alloc([128, 8, T], BF16) for _ in range(2)]
        u = A.alloc([128, 4, L], F32)
        tA = A.alloc([128, L], F32)
        tB = A.alloc([128, L], F32)
        t16 = A.alloc([128, 16], F32)
        gA = [A.alloc([128, T], F32) for _ in range(2)]
        stgf = [A.alloc([128, T], F32) for _ in range(4)]
        cs = [A.alloc([128, 2, T], F32) for _ in range(2)]
        rt = [A.alloc([128, T], F32) for _ in range(4)]
        qs = [A.alloc([128, 8, T], BF16) for _ in range(2)]
        vst = [A.alloc([128, D], BF16) for _ in range(2)]
        frot = Rot(4)
        memset("pool", u[:], 0.0, [("u", g) for g in range(4)])

        def load_x(i):
            b = i % 2
            dma("sp", xin[b][:], x_d[i * T:(i + 1) * T, :].rearrange("(t p) d -> p t d", p=128),
                [], [("xin", b)], ("xin", b))
            dma("sp", cs[b][:, 0, :], cos_d[:, i * T:(i + 1) * T], [], [("cs", b)], ("cs", b))
            dma("sp", cs[b][:, 1, :], sin_d[:, i * T:(i + 1) * T], [], [("cs", b)], ("cs", b))

        cur = [0]

        def proj(col0):
            pk = bank.next()
            hb = cur[0] % 2
            for dc in range(8):
                mm(PS[:, pk, 0:T], Wb[:, dc, col0:col0 + 128], hT2[hb][:, dc, :], dc == 0, dc == 7,
                   [("hT", hb, dc)], [("ps", pk)])
            return pk

        def prologue(i):
            xb = i % 2
            xT, sq, rs, hT = xT2[xb], sq2[xb], rs2[xb], hT2[xb]
            for dc in range(8):
                pk = bank.next()
                for t_ in range(2):
                    tr(PS[:, pk, t_ * 128:(t_ + 1) * 128], xin[xb][:, t_, dc * 128:(dc + 1) * 128], ident[:],
                       [("xin", xb)], [("ps", pk)])
                cp("dve", xT[:, dc, :], PS[:, pk, 0:T], [("ps", pk)], [("xT", xb, dc)])
                act(sq[:, dc, :], xT[:, dc, :], AF.Square, [("xT", xb, dc)], [("sq", xb, dc)])
            pk = bank.next()
            for dc in range(8):
                mm(PS[:, pk, 0:T], onesb[:], sq[:, dc, :], dc == 0, dc == 7, [("sq", xb, dc)], [("ps", pk)])
            act(rs[:], PS[:, pk, 0:T], AF.Sqrt, [("ps", pk)], [("rs", xb)], bias=eps6[:, 0:1])
            recip(rs[:], rs[:], [("rs", xb)], [("rs", xb)])
            for dc in range(8):
                stt("dve", hT[:, dc, :], xT[:, dc, :], vec[:, dc:dc + 1], rs[:], ALU.mult, ALU.mult,
                    [("xT", xb, dc), ("rs", xb)], [("hT", xb, dc)])
            dma("pool", res_view(i, T).rearrange("(c p) t -> p c t", p=128), xT[:],
                [("xT", xb, dc) for dc in range(8)], [], ("xTst", xb))

        load_x(0)
        if NT > 1:
            load_x(1)
        prologue(0)
        for i in range(NT if _DBG.get("stop") != "prep" else 0):
            t0 = i * T
            xb = i % 2
            cur[0] = i
            hT = hT2[xb]
            for g in range(4):
                pk = proj(O_U + g * 128)
                cp("act", u[:, g, 16:L], PS[:, pk, 0:T], [("ps", pk)], [("u", g)])
            if _DBG.get("stop") == "u":
                continue
            for wi, (o_, dst) in enumerate(((O_Q, QT_d), (O_K, KT_d))):
                p0 = proj(o_)
                p1 = proj(o_ + 128)
                cosb = cs[xb][:, 0, :]
                sinb = cs[xb][:, 1, :]
                tt("dve", rt[0][:], PS[:, p0, 0:T], cosb, ALU.mult, [("ps", p0), ("cs", xb)], [("rt", 0)])
                tt("dve", rt[1][:], PS[:, p1, 0:T], sinb, ALU.mult, [("ps", p1), ("cs", xb)], [("rt", 1)])
                tt("pool", qs[wi][:, 0, :], rt[0][:], rt[1][:], ALU.subtract, [("rt", 0), ("rt", 1)], [("qs", wi)])
                tt("dve", rt[2][:], PS[:, p1, 0:T], cosb, ALU.mult, [("ps", p1), ("cs", xb)], [("rt", 2)])
                tt("dve", rt[3][:], PS[:, p0, 0:T], sinb, ALU.mult, [("ps", p0), ("cs", xb)], [("rt", 3)])
                tt("pool", qs[wi][:, 1, :], rt[2][:], rt[3][:], ALU.add, [("rt", 2), ("rt", 3)], [("qs", wi)])
                for j in range(2, 8):
                    pk = proj(o_ + j * 128)
                    cp(("act", "dve")[j % 2], qs[wi][:, j, :], PS[:, pk, 0:T], [("ps", pk)], [("qs", wi)])
                dma("pool", dst.rearrange("(j p) t -> p j t", p=128)[:, :, t0:t0 + T], qs[wi][:],
                    [("qs", wi)], [], ("qs", wi))
            if _DBG.get("stop") == "qk":
                continue
            for g in range(4):
                U = u[:, g, :]
                tt("pool", tA[:, 1:L], U[:, 1:L], U[:, 0:L - 1], ALU.add, [("u", g)], ["tA"])
                win = tA
                if g >= 1:
                    tt("pool", tB[:, 3:L], tA[:, 3:L], tA[:, 1:L - 2], ALU.add, ["tA"], ["tB"])
                    win = tB
                if g >= 2:
                    tt("pool", tA[:, 7:L], tB[:, 7:L], tB[:, 3:L - 4], ALU.add, ["tB"], ["tA"])
                    win = tA
                if g >= 3:
                    tt("pool", tB[:, 15:L], tA[:, 15:L], tA[:, 7:L - 8], ALU.add, ["tA"], ["tB"])
                    win = tB
                wtok = "tA" if win is tA else "tB"
                stt("dve", R1[:, g, :], win[:, 16:L], 1.0 / (2 ** (g + 1)), U[:, 16:L], ALU.mult, ALU.subtract,
                    [wtok, ("u", g)], [("R1", g)])
                if i == 0:
                    tt("pool", t16[:], win[:, 16:32], invc[:, g * 16:(g + 1) * 16], ALU.mult, [wtok, "invc"], ["t16"])
                    tt("pool", R1[:, g, 0:16], t16[:], U[:, 16:32], ALU.subtract, ["t16", ("u", g)], [("R1", g)])
            cp("pool", u[:, :, 0:16], u[:, :, T:T + 16], [("u", g) for g in range(4)], [("u", g) for g in range(4)])
            if _DBG.get("stop") == "pool":
                continue
            for g in range(4):
                pk = bank.next()
                mm(PS[:, pk, 0:T], Wpw[:, g, :], R1[:, g, :], True, True, [("R1", g)], [("ps", pk)])
                ts("dve", R1[:, 4 + g, :], PS[:, pk, 0:T], vec[:, 24 + g:25 + g], None, ALU.mult, None,
                   [("ps", pk)], [("R1", 4 + g)])
            if _DBG.get("stop") == "mix":
                continue
            for t_ in range(2):
                for half in range(2):
                    pk = bank.next()
                    for dc in range(8):
                        mm(PS[:, pk, :], hT[:, dc, t_ * 128:(t_ + 1) * 128],
                           Wb[:, dc, O_V + half * 512:O_V + (half + 1) * 512], dc == 0, dc == 7,
                           [("hT", xb, dc)], [("ps", pk)])
                    cp(("act", "dve")[half], vst[t_][:, half * 512:(half + 1) * 512], PS[:, pk, :],
                       [("ps", pk)], [("vst", t_)])
                dma("pool", V_d[t0 + t_ * 128:t0 + (t_ + 1) * 128, :], vst[t_][:], [("vst", t_)], [], ("vst", t_))
            if i + 2 < NT:
                load_x(i + 2)
            if i + 1 < NT:
                prologue(i + 1)
            for c in range(8):
                pkg = proj(O_GA + c * 128)
                act(gA[c % 2][:], PS[:, pkg, 0:T], AF.Sigmoid, [("ps", pkg)], [("gA", c % 2)], bias=vec[:, 8 + c:9 + c])
                pky = bank.next()
                for g in range(4):
                    mm(PS[:, pky, 0:T], Wpo[:, g, c * 128:(c + 1) * 128], R1[:, 4 + g, :], g == 0, g == 3,
                       [("R1", 4 + g)], [("ps", pky)])
                r = frot.next()
                tt("dve", stgf[r][:], PS[:, pky, 0:T], gA[c % 2][:], ALU.mult, [("ps", pky), ("gA", c % 2)], [("stgf", r)])
                dma("pool", mA_d[c * 128:(c + 1) * 128, t0:t0 + T], stgf[r][:], [("stgf", r)], [], ("stgf", r))
            if _DBG.get("stop") == "ga":
                continue
            for c in range(8):
                pk = proj(O_GB + c * 128)
                r = frot.next()
                act(stgf[r][:], PS[:, pk, 0:T], AF.Sigmoid, [("ps", pk)], [("stgf", r)], bias=vec[:, 16 + c:17 + c])
                dma("pool", gB_d[c * 128:(c + 1) * 128, t0:t0 + T], stgf[r][:], [("stgf", r)], [], ("stgf", r))
        P.barrier()

    if "B" in phases:
        NQB = S // 512
        NKB = S // 128
        B = Arena(nc, PH_BASE)
        qt = [B.alloc([128, S], BF16) for _ in range(2)]
        ktp = [[B.alloc([128, S], BF16) for _ in range(2)] for _ in range(2)]
        vt = [B.alloc([128, NKB, 129], BF16) for _ in range(2)]
        E = [B.alloc([128, 2, 512], BF16) for _ in range(4)]
        att = B.alloc([128, 4, 128], F32)
        sq4 = B.alloc([128, 4, 128], F32)
        ss = B.alloc([128, 4], F32)
        rs4 = B.alloc([128, 4], F32)
        rl = B.alloc([128, 9], F32)
        attb = B.alloc([128, 4, 128], BF16)
        ast = [B.alloc([128, 512], BF16) for _ in range(2)]
        for b in range(2):
            memset("pool", vt[b][:, :, 128:129], 1.0, [("vt", b)])
            for c in range(2):
                memset(("dve", "pool")[c], ktp[b][c][:], 0.0, [("kt", b)])

        def acc(a):
            return PS[:, 4 + a // 3, (a % 3) * 129:(a % 3) * 129 + 129]

        def load_head(h):
            hb = h % 2
            for c in range(2):
                for j in range(8):
                    r0 = j * 128 + (2 * h + c) * 8
                    p0 = c * 64 + j * 8
                    dma("sp", qt[hb][p0:p0 + 8, :], QT_d[r0:r0 + 8, :], [], [("qt", hb)], ("qt", hb))
                    dma("sp", ktp[hb][c][p0:p0 + 8, :], KT_d[r0:r0 + 8, :], [], [("kt", hb)], ("kt", hb))
            vsrc = V_d.rearrange("(k p) f -> p k f", p=128)
            nch = max(1, NKB // 16)
            for q4 in range(nch):
                k0 = q4 * (NKB // nch)
                k1 = (q4 + 1) * (NKB // nch)
                dma("sp", vt[hb][:, k0:k1, 0:128], vsrc[:, k0:k1, h * 128:(h + 1) * 128], [], [("vt", hb)], ("vt", hb))

        steps = [(h, i, j) for h in range(8) for i in range(NQB) for j in range(4 * i + 4)]
        deferred = []
        arot = Rot(2)

        def QK(n):
            h, i, j = steps[n]
            sb, hb = n % 2, h % 2
            jj = j - 4 * i
            n0 = 128 * jj if jj > 0 else 0
            for c in range(2):
                mm(PS[:, 2 * sb + c, n0:512], ktp[hb][c][:, j * 128:(j + 1) * 128],
                   qt[hb][:, i * 512 + n0:(i + 1) * 512], True, True, [("kt", hb), ("qt", hb)], [("S", sb)])

        def EXP(n):
            h, i, j = steps[n]
            sb, eb = n % 2, n % 4
            jj = j - 4 * i
            n0 = 128 * jj if jj > 0 else 0
            act(E[eb][:, :, n0:512], PS[:, 2 * sb:2 * sb + 2, n0:512], AF.Exp, [("S", sb)], [("E", eb)], scale=0.125)
            if jj >= 0:
                memset("pool", E[eb][64:128, :, n0:n0 + 64], 0.0, [("E", eb)])

        def AV(n):
            h, i, j = steps[n]
            eb, hb = n % 4, h % 2
            jj = j - 4 * i
            for t_ in range(max(jj, 0), 4):
                for c in range(2):
                    a = t_ * 2 + c
                    mm(acc(a), E[eb][:, c, t_ * 128:(t_ + 1) * 128], vt[hb][:, j, :],
                       (j == 0 and a % 3 == 0), (j == 4 * i + t_), [("E", eb), ("vt", hb)], [("acc", a // 3)], skip=True)

        def epilogue(h, i):
            for b in range(3):
                ncol = 3 if b < 2 else 2
                recip(rl[:, 3 * b:3 * b + ncol], PS[:, 4 + b, 128:128 + 129 * (ncol - 1) + 1:129], [("acc", b)], ["rl"])
            ts("dve", rl[:, 1:8:2], rl[:, 1:8:2], neglam[:, 0:1], None, ALU.mult, None, ["rl", "neglam"], ["rl"])
            for t_ in range(4):
                ts("dve", att[:, t_, :], acc(2 * t_)[:, 0:128], rl[:, 2 * t_:2 * t_ + 1], None, ALU.mult, None,
                   [("acc", (2 * t_) // 3), "rl"], [("att", t_)])
                stt("dve", att[:, t_, :], acc(2 * t_ + 1)[:, 0:128], rl[:, 2 * t_ + 1:2 * t_ + 2], att[:, t_, :],
                    ALU.mult, ALU.add, [("acc", (2 * t_ + 1) // 3), "rl", ("att", t_)], [("att", t_)])
            atoks = [("att", t_) for t_ in range(4)]
            tt("pool", sq4[:], att[:], att[:], ALU.mult, atoks, ["sq4"])
            rsum(ss[:], sq4[:], ["sq4"], ["ss"])
            act(rs4[:], ss[:], AF.Ln, ["ss"], ["rs4"], bias=eps5[:, 0:1], scale=1.0 / 128.0)
            act(rs4[:], rs4[:], AF.Exp, ["rs4"], ["rs4"], scale=-0.5)
            for t_ in range(4):
                ts("dve", attb[:, t_, :], att[:, t_, :], rs4[:, t_:t_ + 1], None, ALU.mult, None,
                   [("att", t_), "rs4"], [("attb", t_)])

            def late():
                for t_ in range(4):
                    tr(PSb[:, 7, t_ * 128:(t_ + 1) * 128], attb[:, t_, :], identb[:], [("attb", t_)], ["ptr"])
                sa = arot.next()
                ts("dve", ast[sa][:], PSb[:, 7, 0:512], vec[:, 28:29], 0.8, ALU.mult, ALU.mult, ["ptr"], [("ast", sa)])
                dma("pool", aT_d[h * 128:(h + 1) * 128, i * 512:(i + 1) * 512], ast[sa][:], [("ast", sa)], [], ("ast", sa))
            return late

        load_head(0)
        NS = len(steps)
        for n in range(NS + 2):
            if n < NS:
                QK(n)
                EXP(n)
            if n >= 2:
                h, i, j = steps[n - 2]
                AV(n - 2)
                if j == 4 * i + 3:
                    deferred.append([2, epilogue(h, i)])
            if n < NS:
                h, i, j = steps[n]
                if i == 0 and j == 0 and h + 1 < 8 and n >= 0:
                    pending_head = h + 1
            if n >= 1 and n - 1 < NS:
                h, i, j = steps[n - 1]
                if i == 0 and j == 0 and h + 1 < 8:
                    load_head(h + 1)
            for dfr in list(deferred):
                if dfr[0] == 0 or n == NS + 1:
                    dfr[1]()
                    deferred.remove(dfr)
                else:
                    dfr[0] -= 1
        P.barrier()

    if "1" in phases:
        T = 256
        NT = S // T
        C = Arena(nc, PH_BASE)
        Wao = C.alloc([128, 8, D], BF16)
        Wo = C.alloc([128, 8, D], BF16)
        mark = C.cur
        stage = [C.alloc([128, 1024], F32) for _ in range(6)]
        srot = Rot(6)
        for k in range(8):
            load_cast(stage, srot, Wao[:, k, :], w_ao_d[k * 128:(k + 1) * 128, :], 1024)
        for k in range(8):
            load_cast(stage, srot, Wo[:, k, :], w_o_d[k * 128:(k + 1) * 128, :], 1024)
        P.barrier()
        C.cur = mark
        at = [C.alloc([128, 8, T], BF16) for _ in range(2)]
        mAb = [C.alloc([128, T], F32) for _ in range(3)]
        gBb = [C.alloc([128, T], F32) for _ in range(3)]
        xTb = [C.alloc([128, T], F32) for _ in range(3)]
        tmp = [C.alloc([128, T], F32) for _ in range(2)]
        mg = C.alloc([128, 8, T], BF16)
        x1s = [C.alloc([128, T], F32) for _ in range(3)]
        r3 = Rot(3)
        rx = Rot(3)
        ro = Rot(3)

        def load_at(i):
            b = i % 2
            dma("sp", at[b][:], aT_d.rearrange("(k p) t -> p k t", p=128)[:, :, i * T:(i + 1) * T],
                [], [("at", b)], ("at", b))

        load_at(0)
        for i in range(NT):
            t0 = i * T
            ab = i % 2
            if i + 1 < NT:
                load_at(i + 1)
            for c in range(8):
                r = r3.next()
                dma("sp", mAb[r][:], mA_d[c * 128:(c + 1) * 128, t0:t0 + T], [], [("mAb", r)], ("mAb", r))
                dma("sp", gBb[r][:], gB_d[c * 128:(c + 1) * 128, t0:t0 + T], [], [("gBb", r)], ("gBb", r))
                pk = bank.next()
                for k in range(8):
                    mm(PS[:, pk, 0:T], Wao[:, k, c * 128:(c + 1) * 128], at[ab][:, k, :], k == 0, k == 7,
                       [("at", ab)], [("ps", pk)])
                tt("dve", tmp[c % 2][:], PS[:, pk, 0:T], gBb[r][:], ALU.mult, [("ps", pk), ("gBb", r)], [("tmp", c % 2)])
                tt("pool", mg[:, c, :], tmp[c % 2][:], mAb[r][:], ALU.add, [("tmp", c % 2), ("mAb", r)], [("mg", c)])
            for c2 in range(8):
                r = rx.next()
                dma("sp", xTb[r][:], res_view(i, T)[c2 * 128:(c2 + 1) * 128, :], [], [("xTb", r)], ("xTb", r))
                pk = bank.next()
                for c in range(8):
                    mm(PS[:, pk, 0:T], Wo[:, c, c2 * 128:(c2 + 1) * 128], mg[:, c, :], c == 0, c == 7,
                       [("mg", c)], [("ps", pk)])
                o = ro.next()
                tt("dve", x1s[o][:], PS[:, pk, 0:T], xTb[r][:], ALU.add, [("ps", pk), ("xTb", r)], [("x1s", o)])
                dma("pool", res_view(i, T)[c2 * 128:(c2 + 1) * 128, :], x1s[o][:], [("x1s", o)], [], ("x1s", o))
        P.barrier()

    out_keys = []
    if "2" in phases:
        T = 256
        NT = S // T
        C = Arena(nc, PH_BASE)
        Wup = C.alloc([128, 8, DFF], BF16)
        Wdn = C.alloc([128, 32, D], BF16)
        mark = C.cur
        stage = [C.alloc([128, 1024], F32) for _ in range(4)]
        srot = Rot(4)
        for k in range(8):
            for q4 in range(4):
                load_cast(stage, srot, Wup[:, k, q4 * 1024:(q4 + 1) * 1024],
                          w_up_d[k * 128:(k + 1) * 128, q4 * 1024:(q4 + 1) * 1024], 1024)
        for f in range(32):
            load_cast(stage, srot, Wdn[:, f, :], w_dn_d[f * 128:(f + 1) * 128, :], 1024)
        P.barrier()
        C.cur = mark
        x1b = [C.alloc([128, 8, T], F32) for _ in range(2)]
        h2b = [C.alloc([128, 8, T], BF16) for _ in range(2)]
        sqa = C.alloc([128, 8, T], BF16)
        sqb = C.alloc([128, 8, T], BF16)
        rsa = [C.alloc([128, T], F32) for _ in range(2)]
        rsb = C.alloc([128, T], F32)
        aT = C.alloc([128, 32, T], BF16)
        rtmp = [C.alloc([128, T], F32) for _ in range(3)]
        ostg = [C.alloc([128, D], F32) for _ in range(2)]
        rr = Rot(3)

        def head(i):
            p = i % 2
            x1 = x1b[p]
            dma("sp", x1[:], res_view(i, T).rearrange("(k p) t -> p k t", p=128),
                [], [("x1", p, k) for k in range(8)], ("x1", p))
            for k in range(8):
                act(sqa[:, k, :], x1[:, k, :], AF.Square, [("x1", p, k)], [("sqa", k)])
            pk = bank.next()
            for k in range(8):
                mm(PS[:, pk, 0:T], onesb[:], sqa[:, k, :], k == 0, k == 7, [("sqa", k)], [("ps", pk)])
            act(rsa[p][:], PS[:, pk, 0:T], AF.Sqrt, [("ps", pk)], [("rsa", p)], bias=eps6[:, 0:1])
            recip(rsa[p][:], rsa[p][:], [("rsa", p)], [("rsa", p)])
            for k in range(8):
                stt("dve", h2b[p][:, k, :], x1[:, k, :], vec[:, 29 + k:30 + k], rsa[p][:], ALU.mult, ALU.mult,
                    [("x1", p, k), ("rsa", p)], [("h2", p, k)])

        def up_group(i, f):
            p = i % 2
            pk = bank.next()
            for k in range(8):
                mm(PS[:, pk, 0:T], Wup[:, k, f * 128:(f + 1) * 128], h2b[p][:, k, :], k == 0, k == 7,
                   [("h2", p, k)], [("ps", pk)])
            r = rr.next()
            act(rtmp[r][:], PS[:, pk, 0:T], AF.Relu, [("ps", pk)], [("rtmp", r)])
            tt("pool", aT[:, f, :], rtmp[r][:], rtmp[r][:], ALU.mult, [("rtmp", r)], [("aT", f)])

        def down_group(i, c):
            p = i % 2
            pk = bank.next()
            for f in range(32):
                mm(PS[:, pk, 0:T], Wdn[:, f, c * 128:(c + 1) * 128], aT[:, f, :], f == 0, f == 31,
                   [("aT", f)], [("ps", pk)])
            tt("dve", x1b[p][:, c, :], PS[:, pk, 0:T], x1b[p][:, c, :], ALU.add, [("ps", pk), ("x1", p, c)], [("x1", p, c)])

        def tailA(i):
            p = i % 2
            x1 = x1b[p]
            for k in range(8):
                act(sqb[:, k, :], x1[:, k, :], AF.Square, [("x1", p, k)], [("sqb", k)])
            pk = bank.next()
            for k in range(8):
                mm(PS[:, pk, 0:T], onesb[:], sqb[:, k, :], k == 0, k == 7, [("sqb", k)], [("ps", pk)])
            act(rsb[:], PS[:, pk, 0:T], AF.Sqrt, [("ps", pk)], ["rsb"], bias=eps6[:, 0:1])
            recip(rsb[:], rsb[:], ["rsb"], ["rsb"])
            for k in range(8):
                stt("dve", x1[:, k, :], x1[:, k, :], vec[:, 37 + k:38 + k], rsb[:], ALU.mult, ALU.mult,
                    [("x1", p, k), "rsb"], [("x1", p, k)])

        def tailB(i):
            p = i % 2
            x1 = x1b[p]
            t0 = i * T
            for t_ in range(2):
                for half in range(2):
                    pk = bank.next()
                    for kk in range(4):
                        k = half * 4 + kk
                        tr(PS[:, pk, kk * 128:(kk + 1) * 128], x1[:, k, t_ * 128:(t_ + 1) * 128], ident[:],
                           [("x1", p, k)], [("ps", pk)])
                    cp(("act", "dve")[half], ostg[t_][:, half * 512:(half + 1) * 512], PS[:, pk, :],
                       [("ps", pk)], [("ostg", t_)])
                dma("pool", out_d[t0 + t_ * 128:t0 + (t_ + 1) * 128, :], ostg[t_][:], [("ostg", t_)], [], ("ostg", t_))

        head(0)
        for i in range(NT):
            for f in range(32):
                up_group(i, f)
                if i > 0 and f == 7:
                    tailA(i - 1)
                if i > 0 and f == 15:
                    tailB(i - 1)
            for c in range(8):
                down_group(i, c)
                if c == 3 and i + 1 < NT:
                    head(i + 1)
        tailA(NT - 1)
        tailB(NT - 1)
        out_keys = [("ostg", 0), ("ostg", 1)]

    P.emit(final_wait_keys=out_keys)
    return nc


def _consts(S):
    pos = np.arange(S, dtype=np.float32)
    inv = (np.float32(500000.0) ** (-np.arange(0, 16, 2, dtype=np.float32) / np.float32(16))).astype(np.float32)
    ang = (pos[:, None] * inv[None, :]).astype(np.float32)
    cos = np.cos(ang).astype(np.float32)
    sin = np.sin(ang).astype(np.float32)
    cosT = np.ascontiguousarray(np.tile(cos.T, (16, 1)))
    sinT = np.ascontiguousarray(np.tile(sin.T, (16, 1)))
    invc = np.zeros((128, 4, 16), np.float32)
    for g, w in enumerate((2, 4, 8, 16)):
        invc[:, g, :] = 1.0 / np.minimum(np.arange(1, 17), w).astype(np.float32)
    return cosT, sinT, invc.reshape(128, 64), np.eye(128, dtype=np.float32)


def _col(v):
    v = np.asarray(v, np.float32).reshape(-1)
    return v.reshape(-1, 128).T


def make_in_maps(inputs, S, nb):
    f = lambda k: np.ascontiguousarray(np.asarray(inputs[k], dtype=np.float32))
    cosT, sinT, invc, ident = _consts(S)
    vecs = np.concatenate([
        _col(f("g_mix")[0]), _col(f("b_gate")[0, 0]), _col(f("b_gate")[0, 1]), _col(f("pool_scale")[0]),
        _col(f("g_subln")[0]), _col(f("g_mlp")[0]), _col(f("g_final"))], axis=1)
    assert vecs.shape == (128, NV)
    lams = np.concatenate([f("lambda_q1")[0], f("lambda_k1")[0], f("lambda_q2")[0], f("lambda_k2")[0]])
    lams = np.ascontiguousarray(np.tile(lams[None, :], (128, 1)))
    shared = {
        "w_in": f("w_in")[0], "pool_w": f("pool_w")[0], "w_pool_out": f("w_pool_out")[0],
        "w_attn_out": f("w_attn_out")[0], "w_o": f("w_o")[0], "w_up": f("w_up")[0], "w_down": f("w_down")[0],
        "vecs": np.ascontiguousarray(vecs), "lams": lams, "ident": ident, "cosT": cosT, "sinT": sinT, "invc": invc,
    }
    x = f("x")
    maps = []
    for b in range(nb):
        m = dict(shared)
        m["x"] = np.ascontiguousarray(x[b, :S])
        maps.append(m)
    return maps


def kernel(**inputs):
    x = np.asarray(inputs["x"])
    B, S, _ = x.shape
    nc = build_program(S)
    maps = make_in_maps(inputs, S, B)
    res = run_bass_kernel_spmd(nc, maps, core_ids=list(range(B)))
    return np.stack([np.asarray(r["out"], dtype=np.float32) for r in res.results], axis=0)
```

```python
import contextlib
import numpy as np
import concourse.bass as bass
import concourse.mybir as mybir
from concourse.bass_utils import run_bass_kernel_spmd

F32 = mybir.dt.float32
BF16 = mybir.dt.bfloat16
ALU = mybir.AluOpType
AF = mybir.ActivationFunctionType
AX = mybir.AxisListType

D = 1024
INW = 5632
DFF = 4096
O_U, O_Q, O_K, O_V, O_GA, O_GB = 0, 512, 1536, 2560, 3584, 4608
NV = 45
SB_BASE = 17408
SB_LIMIT = 229000

COMPUTE = ("pe", "act", "dve", "pool")
SAME_ENG_RAW = ("act", "dve", "pool")


class Op:
    __slots__ = ("eng", "idx", "fn", "deps", "dma_deps", "sig", "key", "cum", "is_dma")

    def __init__(self, eng, idx, fn):
        self.eng = eng
        self.idx = idx
        self.fn = fn
        self.deps = {}
        self.dma_deps = {}
        self.sig = None
        self.key = None
        self.cum = None
        self.is_dma = False


class Prog:
    def __init__(self, nc):
        self.nc = nc
        self.ops = {e: [] for e in ("pe", "act", "dve", "pool", "sp")}
        self.last_w = {}
        self.readers = {}
        self.dma_cum = {}
        self.dma_keys = []

    def _add_dep(self, op, src):
        if src is None or src is op:
            return
        if src.is_dma:
            op.dma_deps[src.key] = max(op.dma_deps.get(src.key, 0), self.dma_cum[src.key])
        else:
            op.deps[src.eng] = max(op.deps.get(src.eng, -1), src.idx)

    def op(self, eng, fn, reads=(), writes=(), dma_key=None):
        lst = self.ops[eng]
        o = Op(eng, len(lst), fn)
        if dma_key is not None:
            o.is_dma = True
            o.key = dma_key
        def same(src):
            return (not src.is_dma) and src.eng == eng and not o.is_dma

        for r in reads:
            src = self.last_w.get(r)
            if src is None:
                continue
            if same(src):
                if eng in SAME_ENG_RAW:
                    o.deps[eng] = max(o.deps.get(eng, -1), src.idx)
            else:
                self._add_dep(o, src)
        for w in writes:
            src = self.last_w.get(w)
            if src is not None:
                if same(src):
                    if eng in SAME_ENG_RAW:
                        o.deps[eng] = max(o.deps.get(eng, -1), src.idx)
                else:
                    self._add_dep(o, src)
            for rd in self.readers.get(w, ()):
                if same(rd):
                    if eng in SAME_ENG_RAW:
                        o.deps[eng] = max(o.deps.get(eng, -1), rd.idx)
                else:
                    self._add_dep(o, rd)
        if o.is_dma:
            if dma_key not in self.dma_cum:
                self.dma_cum[dma_key] = 0
                self.dma_keys.append(dma_key)
            self.dma_cum[dma_key] += 16
            o.cum = self.dma_cum[dma_key]
        for w in writes:
            self.last_w[w] = o
            self.readers[w] = []
        for r in reads:
            self.readers.setdefault(r, []).append(o)
        lst.append(o)
        return o

    def barrier(self):
        for e in self.ops:
            o = Op(e, len(self.ops[e]), None)
            for e2 in COMPUTE:
                if e2 == e:
                    continue
                j = len(self.ops[e2]) - 1
                while j >= 0 and (self.ops[e2][j].is_dma or self.ops[e2][j].fn is None):
                    j -= 1
                if j >= 0:
                    o.deps[e2] = j
            for k, c in self.dma_cum.items():
                o.dma_deps[k] = c
            self.ops[e].append(o)
        self.last_w = {}
        self.readers = {}

    def emit(self, final_wait_keys=()):
        nc = self.nc
        need = {e: set() for e in self.ops}
        for e, lst in self.ops.items():
            for o in lst:
                for e2, idx in o.deps.items():
                    need[e2].add(idx)
        sigmap = {}
        for e, lst in self.ops.items():
            n = 0
            arr = []
            for o in lst:
                if (not o.is_dma) and o.fn is not None and o.idx in need[e]:
                    n += 1
                    o.sig = n
                arr.append(n)
            sigmap[e] = arr
        _DBG["stats"] = ({e: (sigmap[e][-1] if sigmap[e] else 0) for e in sigmap}, {e: len(self.ops[e]) for e in self.ops}, max(self.dma_cum.values()), len(self.dma_keys))
        with contextlib.ExitStack() as st:
            sems = {e: st.enter_context(nc.semaphore("s_" + e)) for e in COMPUTE}
            dsem = {}
            for i, k in enumerate(self.dma_keys):
                dsem[k] = st.enter_context(nc.semaphore("d%d" % i))
            block = st.enter_context(nc.Block())
            engobj = {"pe": block.tensor, "act": block.scalar, "dve": block.vector,
                      "pool": block.gpsimd, "sp": block.sync}
            for e in ("sp", "pool", "act", "dve", "pe"):
                lst = self.ops[e]

                def body(eng, e=e, lst=lst):
                    known = {}
                    kdma = {}
                    for o in lst:
                        for e2, idx in o.deps.items():
                            v = sigmap[e2][idx]
                            if known.get(e2, 0) < v:
                                eng.wait_ge(sems[e2], v)
                                known[e2] = v
                        for k, c in o.dma_deps.items():
                            if kdma.get(k, 0) < c:
                                eng.wait_ge(dsem[k], c)
                                kdma[k] = c
                        if o.fn is None:
                            continue
                        ins = o.fn(eng)
                        if o.is_dma:
                            ins.then_inc(dsem[o.key], 16)
                        elif o.sig is not None:
                            ins.then_inc(sems[e], 1)
                    if e == "sp":
                        for k in final_wait_keys:
                            eng.wait_ge(dsem[k], self.dma_cum[k])

                engobj[e](body)


_UID = [0]
_DBG = {}


class Arena:
    def __init__(self, nc, base):
        self.nc = nc
        self.cur = base

    def alloc(self, shape, dtype):
        n = 1
        for s in shape[1:]:
            n *= s
        nbytes = n * (4 if dtype == F32 else 2)
        off = self.cur
        self.cur += (nbytes + 63) // 64 * 64
        assert self.cur <= SB_LIMIT, ("SBUF overflow", self.cur)
        _UID[0] += 1
        return self.nc.alloc_sbuf_tensor_at("t%d" % _UID[0], list(shape), dtype, offset=off)


class Rot:
    def __init__(self, n):
        self.n = n
        self.i = -1

    def next(self):
        self.i = (self.i + 1) % self.n
        return self.i


def build_program(S, dbg=False, phases="AB12"):
    nc = bass.Bass("TRN2", target_bir_lowering=False)
    P = Prog(nc)
    skind = "ExternalOutput" if dbg else "Internal"

    def din(name, shape, dt=F32):
        return nc.dram_tensor(name, list(shape), dt, kind="ExternalInput").ap()

    x_d = din("x", [S, D])
    w_in_d = din("w_in", [D, INW])
    pool_w_d = din("pool_w", [4, 128, 128])
    w_po_d = din("w_pool_out", [512, D])
    w_ao_d = din("w_attn_out", [D, D])
    w_o_d = din("w_o", [D, D])
    w_up_d = din("w_up", [D, DFF])
    w_dn_d = din("w_down", [DFF, D])
    vecs_d = din("vecs", [128, NV])
    lams_d = din("lams", [128, 256])
    ident_d = din("ident", [128, 128])
    cos_d = din("cosT", [128, S])
    sin_d = din("sinT", [128, S])
    invc_d = din("invc", [128, 64])
    out_d = nc.dram_tensor("out", [S, D], F32, kind="ExternalOutput").ap()

    QT_d = nc.dram_tensor("QT_s", [D, S], BF16, kind=skind).ap()
    KT_d = nc.dram_tensor("KT_s", [D, S], BF16, kind=skind).ap()
    V_d = nc.dram_tensor("V_s", [S, D], BF16, kind=skind).ap()
    mA_d = nc.dram_tensor("mA_s", [D, S], F32, kind=skind).ap()
    gB_d = nc.dram_tensor("gB_s", [D, S], F32, kind=skind).ap()
    aT_d = nc.dram_tensor("attT_s", [D, S], BF16, kind=skind).ap()

    def res_view(i, T):
        return out_d[i * T:(i + 1) * T, :].rearrange("a (b t) -> (a b) t", t=T)

    PSh = nc.alloc_psum_tensor("PS", [128, 8, 512], F32)
    PS = PSh
    PSb = PSh.bitcast(BF16)
    bank = Rot(8)

    def dma(q, out, in_, reads, writes, key):
        P.op(q, lambda e: e.dma_start(out=out, in_=in_), reads=reads, writes=writes, dma_key=key)

    def mm(out, lhsT, rhs, start, stop, reads, writes, skip=False):
        P.op("pe", lambda e: e.matmul(out, lhsT=lhsT, rhs=rhs, start=start, stop=stop, skip_group_check=skip),
             reads=reads, writes=writes)

    def tr(out, in_, idn, reads, writes):
        P.op("pe", lambda e: e.transpose(out=out, in_=in_, identity=idn), reads=reads, writes=writes)

    def act(out, in_, func, reads, writes, bias=None, scale=1.0):
        if bias is None:
            P.op("act", lambda e: e.activation(out=out, in_=in_, func=func, scale=scale), reads=reads, writes=writes)
        else:
            P.op("act", lambda e: e.activation(out=out, in_=in_, func=func, bias=bias, scale=scale),
                 reads=reads, writes=writes)

    def cp(eng, out, in_, reads, writes):
        if eng == "act":
            P.op("act", lambda e: e.copy(out=out, in_=in_), reads=reads, writes=writes)
        else:
            P.op(eng, lambda e: e.tensor_copy(out=out, in_=in_), reads=reads, writes=writes)

    def tt(eng, out, in0, in1, op, reads, writes):
        P.op(eng, lambda e: e.tensor_tensor(out=out, in0=in0, in1=in1, op=op), reads=reads, writes=writes)

    def ts(eng, out, in0, s1, s2, op0, op1, reads, writes):
        if s2 is None:
            P.op(eng, lambda e: e.tensor_scalar(out=out, in0=in0, scalar1=s1, scalar2=None, op0=op0),
                 reads=reads, writes=writes)
        else:
            P.op(eng, lambda e: e.tensor_scalar(out=out, in0=in0, scalar1=s1, scalar2=s2, op0=op0, op1=op1),
                 reads=reads, writes=writes)

    def stt(eng, out, in0, scalar, in1, op0, op1, reads, writes):
        P.op(eng, lambda e: e.scalar_tensor_tensor(out=out, in0=in0, scalar=scalar, in1=in1, op0=op0, op1=op1),
             reads=reads, writes=writes)

    def memset(eng, ap, val, writes):
        P.op(eng, lambda e: e.memset(ap, val), writes=writes)

    def recip(out, in_, reads, writes):
        P.op("dve", lambda e: e.reciprocal(out=out, in_=in_), reads=reads, writes=writes)

    def rsum(out, in_, reads, writes):
        P.op("dve", lambda e: e.tensor_reduce(out=out, in_=in_, axis=AX.X, op=ALU.add), reads=reads, writes=writes)

    A0 = Arena(nc, SB_BASE)
    vec = A0.alloc([128, NV], F32)
    ident = A0.alloc([128, 128], F32)
    identb = A0.alloc([128, 128], BF16)
    onesb = A0.alloc([128, 128], BF16)
    lam_t = A0.alloc([128, 256], F32)
    lam_p = A0.alloc([128, 64], F32)
    lam_s = A0.alloc([128, 4], F32)
    neglam = A0.alloc([128, 1], F32)
    eps6 = A0.alloc([128, 1], F32)
    eps5 = A0.alloc([128, 1], F32)
    invc = A0.alloc([128, 64], F32)
    PH_BASE = A0.cur

    dma("sp", vec[:], vecs_d, [], ["vec"], "c_vec")
    dma("sp", ident[:], ident_d, [], ["ident"], "c_ident")
    dma("sp", lam_t[:], lams_d, [], ["lam_t"], "c_lam")
    dma("sp", invc[:], invc_d, [], ["invc"], "c_invc")
    memset("dve", onesb[:], 1.0 / 1024.0, ["onesb"])
    memset("dve", eps6[:], 1e-6, ["eps6"])
    memset("dve", eps5[:], 1e-5, ["eps5"])
    cp("dve", identb[:], ident[:], ["ident"], ["identb"])
    for k in range(2):
        tt("dve", lam_p[:], lam_t[:, 128 * k:128 * k + 64], lam_t[:, 128 * k + 64:128 * k + 128], ALU.mult,
           ["lam_t"], ["lam_p"])
        rsum(lam_s[:, k:k + 1], lam_p[:], ["lam_p"], ["lam_s"])
    act(lam_s[:, 2:4], lam_s[:, 0:2], AF.Exp, ["lam_s"], ["lam_e"])
    tt("dve", neglam[:], lam_s[:, 3:4], lam_s[:, 2:3], ALU.subtract, ["lam_e"], ["neglam"])
    ts("dve", neglam[:], neglam[:], -0.2, None, ALU.add, None, ["neglam"], ["neglam"])

    cast_eng = Rot(3)
    CAST_ENGS = ("dve", "pool", "act")

    def load_cast(stage, srot, dst, src, ncols, perm=False, src_view=None):
        sl = srot.next()
        st_ap = stage[sl][:, 0:ncols] if src_view is None else src_view(stage[sl])
        dma("sp", st_ap, src, [], [("stage", sl)], ("stage", sl))
        if perm:
            eng = ("dve", "pool")[cast_eng.next() % 2]
            iv = stage[sl][:, 0:1024].rearrange("p (hc j dd) -> p j hc dd", hc=16, j=8, dd=8)
            ov = dst.rearrange("p (j hc dd) -> p j hc dd", j=8, hc=16, dd=8)
            _UID[0] += 1
            cp(eng, ov, iv, [("stage", sl)], [("W", _UID[0])])
        else:
            eng = CAST_ENGS[cast_eng.next()]
            _UID[0] += 1
            cp(eng, dst, stage[sl][:, 0:ncols], [("stage", sl)], [("W", _UID[0])])

    if "A" in phases:
        T = 256
        NT = S // T
        L = 16 + T
        A = Arena(nc, PH_BASE)
        Wb = A.alloc([128, 8, INW], BF16)
        Wpo = A.alloc([128, 4, D], BF16)
        Wpw = A.alloc([128, 4, 128], BF16)
        mark = A.cur
        stage = [A.alloc([128, 1024], F32) for _ in range(6)]
        srot = Rot(6)
        for dc in range(8):
            rows = w_in_d[dc * 128:(dc + 1) * 128, :]
            for (o, n, perm) in ((O_U, 512, False), (O_Q, 1024, True), (O_K, 1024, True),
                                 (O_V, 1024, False), (O_GA, 1024, False), (O_GB, 1024, False)):
                load_cast(stage, srot, Wb[:, dc, o:o + n], rows[:, o:o + n], n, perm)
        for g in range(4):
            load_cast(stage, srot, Wpo[:, g, :], w_po_d[g * 128:(g + 1) * 128, :], 1024)
        load_cast(stage, srot, Wpw[:].rearrange("p g q -> p (g q)"), pool_w_d.rearrange("g p q -> p g q"), 512,
                  src_view=lambda s: s[:, 0:512].rearrange("p (g q) -> p g q", g=4))
        P.barrier()
        A.cur = mark
        xin = [A.alloc([128, 2, D], F32) for _ in range(2)]
        xT2 = [A.alloc([128, 8, T], F32) for _ in range(2)]
        sq2 = [A.alloc([128, 8, T], BF16) for _ in range(2)]
        R1 = A.alloc([128, 8, T], BF16)
        rs2 = [A.alloc([128, T], F32) for _ in range(2)]
        hT2 = [A.alloc([128, 8, T], BF16) for _ in range(2)]
        u = A.alloc([128, 4, L], F32)
        tA = A.alloc([128, L], F32)
        tB = A.alloc([128, L], F32)
        t16 = A.alloc([128, 16], F32)
        gA = [A.alloc([128, T], F32) for _ in range(3)]
        stgf = [A.alloc([128, T], F32) for _ in range(8)]
        cs = [A.alloc([128, 2, T], F32) for _ in range(2)]
        rt = [A.alloc([128, T], F32) for _ in range(4)]
        qs = [A.alloc([128, 8, T], BF16) for _ in range(2)]
        vst = [A.alloc([128, D], BF16) for _ in range(2)]
        frot = Rot(8)
        memset("pool", u[:], 0.0, [("u", g) for g in range(4)])

        def load_x(i):
            b = i % 2
            dma("sp", xin[b][:], x_d[i * T:(i + 1) * T, :].rearrange("(t p) d -> p t d", p=128),
                [], [("xin", b)], ("xin", b))
            dma("sp", cs[b][:, 0, :], cos_d[:, i * T:(i + 1) * T], [], [("cs", b)], ("cs", b))
            dma("sp", cs[b][:, 1, :], sin_d[:, i * T:(i + 1) * T], [], [("cs", b)], ("cs", b))

        cur = [0]

        def proj(col0):
            pk = bank.next()
            hb = cur[0] % 2
            for dc in range(8):
                mm(PS[:, pk, 0:T], Wb[:, dc, col0:col0 + 128], hT2[hb][:, dc, :], dc == 0, dc == 7,
                   [("hT", hb, dc)], [("ps", pk)])
            return pk

        def prologue_a(i):
            xb = i % 2
            xT, sq = xT2[xb], sq2[xb]
            for dc in range(8):
                pk = bank.next()
                for t_ in range(2):
                    tr(PS[:, pk, t_ * 128:(t_ + 1) * 128], xin[xb][:, t_, dc * 128:(dc + 1) * 128], ident[:],
                       [("xin", xb)], [("ps", pk)])
                cp("dve", xT[:, dc, :], PS[:, pk, 0:T], [("ps", pk)], [("xT", xb, dc)])
                act(sq[:, dc, :], xT[:, dc, :], AF.Square, [("xT", xb, dc)], [("sq", xb, dc)])
            dma("pool", res_view(i, T).rearrange("(c p) t -> p c t", p=128), xT[:],
                [("xT", xb, dc) for dc in range(8)], [], ("xTst", xb))

        def prologue_b(i):
            xb = i % 2
            xT, sq, rs, hT = xT2[xb], sq2[xb], rs2[xb], hT2[xb]
            pk = bank.next()
            for dc in range(8):
                mm(PS[:, pk, 0:T], onesb[:], sq[:, dc, :], dc == 0, dc == 7, [("sq", xb, dc)], [("ps", pk)])
            act(rs[:], PS[:, pk, 0:T], AF.Sqrt, [("ps", pk)], [("rs", xb)], bias=eps6[:, 0:1])
            recip(rs[:], rs[:], [("rs", xb)], [("rs", xb)])
            for dc in range(8):
                stt("dve", hT[:, dc, :], xT[:, dc, :], vec[:, dc:dc + 1], rs[:], ALU.mult, ALU.mult,
                    [("xT", xb, dc), ("rs", xb)], [("hT", xb, dc)])

        load_x(0)
        if NT > 1:
            load_x(1)
        prologue_a(0)
        prologue_b(0)
        for i in range(NT if _DBG.get("stop") != "prep" else 0):
            t0 = i * T
            xb = i % 2
            cur[0] = i
            hT = hT2[xb]
            for g in range(4):
                pk = proj(O_U + g * 128)
                cp("act", u[:, g, 16:L], PS[:, pk, 0:T], [("ps", pk)], [("u", g)])
            if _DBG.get("stop") == "u":
                continue
            for wi, (o_, dst) in enumerate(((O_Q, QT_d), (O_K, KT_d))):
                p0 = proj(o_)
                p1 = proj(o_ + 128)
                cosb = cs[xb][:, 0, :]
                sinb = cs[xb][:, 1, :]
                tt("dve", rt[0][:], PS[:, p0, 0:T], cosb, ALU.mult, [("ps", p0), ("cs", xb)], [("rt", 0)])
                tt("dve", rt[1][:], PS[:, p1, 0:T], sinb, ALU.mult, [("ps", p1), ("cs", xb)], [("rt", 1)])
                tt("dve", qs[wi][:, 0, :], rt[0][:], rt[1][:], ALU.subtract, [("rt", 0), ("rt", 1)], [("qs", wi)])
                tt("dve", rt[2][:], PS[:, p1, 0:T], cosb, ALU.mult, [("ps", p1), ("cs", xb)], [("rt", 2)])
                tt("dve", rt[3][:], PS[:, p0, 0:T], sinb, ALU.mult, [("ps", p0), ("cs", xb)], [("rt", 3)])
                tt("dve", qs[wi][:, 1, :], rt[2][:], rt[3][:], ALU.add, [("rt", 2), ("rt", 3)], [("qs", wi)])
                for j in range(2, 8):
                    pk = proj(o_ + j * 128)
                    cp(("act", "dve")[j % 2], qs[wi][:, j, :], PS[:, pk, 0:T], [("ps", pk)], [("qs", wi)])
                dma("pool", dst.rearrange("(j p) t -> p j t", p=128)[:, :, t0:t0 + T], qs[wi][:],
                    [("qs", wi)], [], ("qs", wi))
            if _DBG.get("stop") == "qk":
                continue
            for g in range(4):
                U = u[:, g, :]
                tt("pool", tA[:, 1:L], U[:, 1:L], U[:, 0:L - 1], ALU.add, [("u", g)], ["tA"])
                win = tA
                if g >= 1:
                    tt("pool", tB[:, 3:L], tA[:, 3:L], tA[:, 1:L - 2], ALU.add, ["tA"], ["tB"])
                    win = tB
                if g >= 2:
                    tt("pool", tA[:, 7:L], tB[:, 7:L], tB[:, 3:L - 4], ALU.add, ["tB"], ["tA"])
                    win = tA
                if g >= 3:
                    tt("pool", tB[:, 15:L], tA[:, 15:L], tA[:, 7:L - 8], ALU.add, ["tA"], ["tB"])
                    win = tB
                wtok = "tA" if win is tA else "tB"
                stt("dve", R1[:, g, :], win[:, 16:L], 1.0 / (2 ** (g + 1)), U[:, 16:L], ALU.mult, ALU.subtract,
                    [wtok, ("u", g)], [("R1", g)])
                if i == 0:
                    tt("pool", t16[:], win[:, 16:32], invc[:, g * 16:(g + 1) * 16], ALU.mult, [wtok, "invc"], ["t16"])
                    tt("pool", R1[:, g, 0:16], t16[:], U[:, 16:32], ALU.subtract, ["t16", ("u", g)], [("R1", g)])
            cp("pool", u[:, :, 0:16], u[:, :, T:T + 16], [("u", g) for g in range(4)], [("u", g) for g in range(4)])
            if _DBG.get("stop") == "mix":
                continue
            for t_ in range(2):
                for half in range(2):
                    pk = bank.next()
                    for dc in range(8):
                        mm(PS[:, pk, :], hT[:, dc, t_ * 128:(t_ + 1) * 128],
                           Wb[:, dc, O_V + half * 512:O_V + (half + 1) * 512], dc == 0, dc == 7,
                           [("hT", xb, dc)], [("ps", pk)])
                    cp(("act", "dve")[half], vst[t_][:, half * 512:(half + 1) * 512], PS[:, pk, :],
                       [("ps", pk)], [("vst", t_)])
                dma("pool", V_d[t0 + t_ * 128:t0 + (t_ + 1) * 128, :], vst[t_][:], [("vst", t_)], [], ("vst", t_))
            if _DBG.get("stop") == "ga":
                continue
            for c in range(8):
                pk = proj(O_GB + c * 128)
                r = frot.next()
                act(stgf[r][:], PS[:, pk, 0:T], AF.Sigmoid, [("ps", pk)], [("stgf", r)], bias=vec[:, 16 + c:17 + c])
                dma("pool", gB_d[c * 128:(c + 1) * 128, t0:t0 + T], stgf[r][:], [("stgf", r)], [], ("stgf", r))
            if _DBG.get("stop") == "pool":
                continue
            for g in range(4):
                pk = bank.next()
                mm(PS[:, pk, 0:T], Wpw[:, g, :], R1[:, g, :], True, True, [("R1", g)], [("ps", pk)])
                ts("dve", R1[:, 4 + g, :], PS[:, pk, 0:T], vec[:, 24 + g:25 + g], None, ALU.mult, None,
                   [("ps", pk)], [("R1", 4 + g)])
            if i + 2 < NT:
                load_x(i + 2)
            if i + 1 < NT:
                prologue_a(i + 1)
            for c in range(8):
                if c == 3 and i + 1 < NT:
                    prologue_b(i + 1)
                pkg = proj(O_GA + c * 128)
                act(gA[c % 3][:], PS[:, pkg, 0:T], AF.Sigmoid, [("ps", pkg)], [("gA", c % 3)], bias=vec[:, 8 + c:9 + c])
                pky = bank.next()
                for g in range(4):
                    mm(PS[:, pky, 0:T], Wpo[:, g, c * 128:(c + 1) * 128], R1[:, 4 + g, :], g == 0, g == 3,
                       [("R1", 4 + g)], [("ps", pky)])
                r = frot.next()
                tt("dve", stgf[r][:], PS[:, pky, 0:T], gA[c % 3][:], ALU.mult, [("ps", pky), ("gA", c % 3)], [("stgf", r)])
                dma("pool", mA_d[c * 128:(c + 1) * 128, t0:t0 + T], stgf[r][:], [("stgf", r)], [], ("stgf", r))
        P.barrier()

    if "B" in phases:
        NQB = S // 512
        NKB = S // 128
        B = Arena(nc, PH_BASE)
        qt = [B.alloc([128, S], BF16) for _ in range(2)]
        ktp = [[B.alloc([128, S], BF16) for _ in range(2)] for _ in range(2)]
        vt = [B.alloc([128, NKB, 129], BF16) for _ in range(2)]
        E = [B.alloc([128, 2, 512], BF16) for _ in range(4)]
        att = B.alloc([128, 4, 128], F32)
        sq4 = B.alloc([128, 4, 128], F32)
        ss = B.alloc([128, 4], F32)
        rs4 = B.alloc([128, 4], F32)
        rl = B.alloc([128, 9], F32)
        attb = B.alloc([128, 4, 128], BF16)
        ast = [B.alloc([128, 512], BF16) for _ in range(2)]
        for b in range(2):
            memset("pool", vt[b][:, :, 128:129], 1.0, [("vt", b)])
            for c in range(2):
                memset(("dve", "pool")[c], ktp[b][c][:], 0.0, [("kt", b)])

        def acc(a):
            return PS[:, 4 + a // 3, (a % 3) * 129:(a % 3) * 129 + 129]

        def load_head(h):
            hb = h % 2
            for c in range(2):
                for j in range(8):
                    r0 = j * 128 + (2 * h + c) * 8
                    p0 = c * 64 + j * 8
                    dma("sp", qt[hb][p0:p0 + 8, :], QT_d[r0:r0 + 8, :], [], [("qt", hb)], ("qt", hb))
                    dma("sp", ktp[hb][c][p0:p0 + 8, :], KT_d[r0:r0 + 8, :], [], [("kt", hb)], ("kt", hb))
            vsrc = V_d.rearrange("(k p) f -> p k f", p=128)
            nch = max(1, NKB // 16)
            for q4 in range(nch):
                k0 = q4 * (NKB // nch)
                k1 = (q4 + 1) * (NKB // nch)
                dma("sp", vt[hb][:, k0:k1, 0:128], vsrc[:, k0:k1, h * 128:(h + 1) * 128], [], [("vt", hb)], ("vt", hb))

        steps = [(h, i, j) for h in range(8) for i in range(NQB) for j in range(4 * i + 4)]
        deferred = []
        arot = Rot(2)

        def QK(n):
            h, i, j = steps[n]
            sb, hb = n % 2, h % 2
            jj = j - 4 * i
            n0 = 128 * jj if jj > 0 else 0
            for c in range(2):
                mm(PS[:, 2 * sb + c, n0:512], ktp[hb][c][:, j * 128:(j + 1) * 128],
                   qt[hb][:, i * 512 + n0:(i + 1) * 512], True, True, [("kt", hb), ("qt", hb)], [("S", sb)])

        def EXP(n):
            h, i, j = steps[n]
            sb, eb = n % 2, n % 4
            jj = j - 4 * i
            n0 = 128 * jj if jj > 0 else 0
            act(E[eb][:, :, n0:512], PS[:, 2 * sb:2 * sb + 2, n0:512], AF.Exp, [("S", sb)], [("E", eb)], scale=0.125)
            if jj >= 0:
                memset("pool", E[eb][64:128, :, n0:n0 + 64], 0.0, [("E", eb)])

        def AV(n):
            h, i, j = steps[n]
            eb, hb = n % 4, h % 2
            jj = j - 4 * i
            for t_ in range(max(jj, 0), 4):
                for c in range(2):
                    a = t_ * 2 + c
                    mm(acc(a), E[eb][:, c, t_ * 128:(t_ + 1) * 128], vt[hb][:, j, :],
                       (j == 0 and a % 3 == 0), (j == 4 * i + t_), [("E", eb), ("vt", hb)], [("acc", a // 3)], skip=True)

        def epilogue(h, i):
            for b in range(3):
                ncol = 3 if b < 2 else 2
                recip(rl[:, 3 * b:3 * b + ncol], PS[:, 4 + b, 128:128 + 129 * (ncol - 1) + 1:129], [("acc", b)], ["rl"])
            ts("dve", rl[:, 1:8:2], rl[:, 1:8:2], neglam[:, 0:1], None, ALU.mult, None, ["rl", "neglam"], ["rl"])
            for t_ in range(4):
                ts("dve", att[:, t_, :], acc(2 * t_)[:, 0:128], rl[:, 2 * t_:2 * t_ + 1], None, ALU.mult, None,
                   [("acc", (2 * t_) // 3), "rl"], [("att", t_)])
                stt("dve", att[:, t_, :], acc(2 * t_ + 1)[:, 0:128], rl[:, 2 * t_ + 1:2 * t_ + 2], att[:, t_, :],
                    ALU.mult, ALU.add, [("acc", (2 * t_ + 1) // 3), "rl", ("att", t_)], [("att", t_)])
            atoks = [("att", t_) for t_ in range(4)]
            tt("pool", sq4[:], att[:], att[:], ALU.mult, atoks, ["sq4"])
            rsum(ss[:], sq4[:], ["sq4"], ["ss"])
            act(rs4[:], ss[:], AF.Ln, ["ss"], ["rs4"], bias=eps5[:, 0:1], scale=1.0 / 128.0)
            act(rs4[:], rs4[:], AF.Exp, ["rs4"], ["rs4"], scale=-0.5)
            for t_ in range(4):
                ts("dve", attb[:, t_, :], att[:, t_, :], rs4[:, t_:t_ + 1], None, ALU.mult, None,
                   [("att", t_), "rs4"], [("attb", t_)])

            def late():
                for t_ in range(4):
                    tr(PSb[:, 7, t_ * 128:(t_ + 1) * 128], attb[:, t_, :], identb[:], [("attb", t_)], ["ptr"])
                sa = arot.next()
                ts("dve", ast[sa][:], PSb[:, 7, 0:512], vec[:, 28:29], 0.8, ALU.mult, ALU.mult, ["ptr"], [("ast", sa)])
                dma("pool", aT_d[h * 128:(h + 1) * 128, i * 512:(i + 1) * 512], ast[sa][:], [("ast", sa)], [], ("ast", sa))
            return late

        load_head(0)
        NS = len(steps)
        for n in range(NS + 2):
            if n < NS:
                QK(n)
                EXP(n)
            if n >= 2:
                h, i, j = steps[n - 2]
                AV(n - 2)
                if j == 4 * i + 3:
                    deferred.append([2, epilogue(h, i)])
            if n < NS:
                h, i, j = steps[n]
                if i == 0 and j == 0 and h + 1 < 8 and n >= 0:
                    pending_head = h + 1
            if n >= 1 and n - 1 < NS:
                h, i, j = steps[n - 1]
                if i == 0 and j == 0 and h + 1 < 8:
                    load_head(h + 1)
            for dfr in list(deferred):
                if dfr[0] == 0 or n == NS + 1:
                    dfr[1]()
                    deferred.remove(dfr)
                else:
                    dfr[0] -= 1
        P.barrier()

    if "1" in phases:
        T = 256
        NT = S // T
        C = Arena(nc, PH_BASE)
        Wao = C.alloc([128, 8, D], BF16)
        Wo = C.alloc([128, 8, D], BF16)
        mark = C.cur
        stage = [C.alloc([128, 1024], F32) for _ in range(6)]
        srot = Rot(6)
        for k in range(8):
            load_cast(stage, srot, Wao[:, k, :], w_ao_d[k * 128:(k + 1) * 128, :], 1024)
        for k in range(8):
            load_cast(stage, srot, Wo[:, k, :], w_o_d[k * 128:(k + 1) * 128, :], 1024)
        P.barrier()
        C.cur = mark
        at = [C.alloc([128, 8, T], BF16) for _ in range(2)]
        mAb = [C.alloc([128, T], F32) for _ in range(3)]
        gBb = [C.alloc([128, T], F32) for _ in range(3)]
        xTb = [C.alloc([128, T], F32) for _ in range(3)]
        tmp = [C.alloc([128, T], F32) for _ in range(2)]
        mg = C.alloc([128, 8, T], BF16)
        x1s = [C.alloc([128, T], F32) for _ in range(3)]
        r3 = Rot(3)
        rx = Rot(3)
        ro = Rot(3)

        def load_at(i):
            b = i % 2
            dma("sp", at[b][:], aT_d.rearrange("(k p) t -> p k t", p=128)[:, :, i * T:(i + 1) * T],
                [], [("at", b)], ("at", b))

        def ya_group(i, c):
            ab = i % 2
            t0 = i * T
            r = r3.next()
            dma("sp", mAb[r][:], mA_d[c * 128:(c + 1) * 128, t0:t0 + T], [], [("mAb", r)], ("mAb", r))
            dma("sp", gBb[r][:], gB_d[c * 128:(c + 1) * 128, t0:t0 + T], [], [("gBb", r)], ("gBb", r))
            pk = bank.next()
            for k in range(8):
                mm(PS[:, pk, 0:T], Wao[:, k, c * 128:(c + 1) * 128], at[ab][:, k, :], k == 0, k == 7,
                   [("at", ab)], [("ps", pk)])
            tt("dve", tmp[c % 2][:], PS[:, pk, 0:T], gBb[r][:], ALU.mult, [("ps", pk), ("gBb", r)], [("tmp", c % 2)])
            tt("pool", mg2[ab][:, c, :], tmp[c % 2][:], mAb[r][:], ALU.add, [("tmp", c % 2), ("mAb", r)], [("mg", ab, c)])

        def wo_group(i, c2):
            ab = i % 2
            r = rx.next()
            dma("sp", xTb[r][:], res_view(i, T)[c2 * 128:(c2 + 1) * 128, :], [], [("xTb", r)], ("xTb", r))
            pk = bank.next()
            for c in range(8):
                mm(PS[:, pk, 0:T], Wo[:, c, c2 * 128:(c2 + 1) * 128], mg2[ab][:, c, :], c == 0, c == 7,
                   [("mg", ab, c)], [("ps", pk)])
            o = ro.next()
            tt("dve", x1s[o][:], PS[:, pk, 0:T], xTb[r][:], ALU.add, [("ps", pk), ("xTb", r)], [("x1s", o)])
            dma("pool", res_view(i, T)[c2 * 128:(c2 + 1) * 128, :], x1s[o][:], [("x1s", o)], [], ("x1s", o))

        mg2 = [mg, C.alloc([128, 8, T], BF16)]
        load_at(0)
        for i in range(NT + 1):
            if i + 1 < NT:
                load_at(i + 1)
            for c in range(8):
                if i < NT:
                    ya_group(i, c)
                if i >= 1:
                    wo_group(i - 1, c)
        P.barrier()

    out_keys = []
    if "2" in phases:
        T = 256
        NT = S // T
        C = Arena(nc, PH_BASE)
        Wup = C.alloc([128, 8, DFF], BF16)
        Wdn = C.alloc([128, 32, D], BF16)
        mark = C.cur
        stage = [C.alloc([128, 1024], F32) for _ in range(4)]
        srot = Rot(4)
        for k in range(8):
            for q4 in range(4):
                load_cast(stage, srot, Wup[:, k, q4 * 1024:(q4 + 1) * 1024],
                          w_up_d[k * 128:(k + 1) * 128, q4 * 1024:(q4 + 1) * 1024], 1024)
        for f in range(32):
            load_cast(stage, srot, Wdn[:, f, :], w_dn_d[f * 128:(f + 1) * 128, :], 1024)
        P.barrier()
        C.cur = mark
        x1b = [C.alloc([128, 8, T], F32) for _ in range(2)]
        h2b = [C.alloc([128, 8, T], BF16) for _ in range(2)]
        sqa = C.alloc([128, 8, T], BF16)
        sqb = C.alloc([128, 8, T], BF16)
        rsa = [C.alloc([128, T], F32) for _ in range(2)]
        rsb = C.alloc([128, T], F32)
        aT = C.alloc([128, 32, T], BF16)
        rtmp = [C.alloc([128, T], F32) for _ in range(3)]
        ostg = [C.alloc([128, D], F32) for _ in range(2)]
        rr = Rot(3)

        def head(i):
            p = i % 2
            x1 = x1b[p]
            dma("sp", x1[:], res_view(i, T).rearrange("(k p) t -> p k t", p=128),
                [], [("x1", p, k) for k in range(8)], ("x1", p))
            for k in range(8):
                act(sqa[:, k, :], x1[:, k, :], AF.Square, [("x1", p, k)], [("sqa", k)])
            pk = bank.next()
            for k in range(8):
                mm(PS[:, pk, 0:T], onesb[:], sqa[:, k, :], k == 0, k == 7, [("sqa", k)], [("ps", pk)])
            act(rsa[p][:], PS[:, pk, 0:T], AF.Sqrt, [("ps", pk)], [("rsa", p)], bias=eps6[:, 0:1])
            recip(rsa[p][:], rsa[p][:], [("rsa", p)], [("rsa", p)])
            for k in range(8):
                stt("dve", h2b[p][:, k, :], x1[:, k, :], vec[:, 29 + k:30 + k], rsa[p][:], ALU.mult, ALU.mult,
                    [("x1", p, k), ("rsa", p)], [("h2", p, k)])

        def up_group(i, f):
            p = i % 2
            pk = bank.next()
            for k in range(8):
                mm(PS[:, pk, 0:T], Wup[:, k, f * 128:(f + 1) * 128], h2b[p][:, k, :], k == 0, k == 7,
                   [("h2", p, k)], [("ps", pk)])
            r = rr.next()
            act(rtmp[r][:], PS[:, pk, 0:T], AF.Relu, [("ps", pk)], [("rtmp", r)])
            tt("pool", aT[:, f, :], rtmp[r][:], rtmp[r][:], ALU.mult, [("rtmp", r)], [("aT", f)])

        def down_group(i, c):
            p = i % 2
            pk = bank.next()
            for f in range(32):
                mm(PS[:, pk, 0:T], Wdn[:, f, c * 128:(c + 1) * 128], aT[:, f, :], f == 0, f == 31,
                   [("aT", f)], [("ps", pk)])
            tt("dve", x1b[p][:, c, :], PS[:, pk, 0:T], x1b[p][:, c, :], ALU.add, [("ps", pk), ("x1", p, c)], [("x1", p, c)])

        def tailA(i):
            p = i % 2
            x1 = x1b[p]
            for k in range(8):
                act(sqb[:, k, :], x1[:, k, :], AF.Square, [("x1", p, k)], [("sqb", k)])
            pk = bank.next()
            for k in range(8):
                mm(PS[:, pk, 0:T], onesb[:], sqb[:, k, :], k == 0, k == 7, [("sqb", k)], [("ps", pk)])
            act(rsb[:], PS[:, pk, 0:T], AF.Sqrt, [("ps", pk)], ["rsb"], bias=eps6[:, 0:1])
            recip(rsb[:], rsb[:], ["rsb"], ["rsb"])
            for k in range(8):
                stt("dve", x1[:, k, :], x1[:, k, :], vec[:, 37 + k:38 + k], rsb[:], ALU.mult, ALU.mult,
                    [("x1", p, k), "rsb"], [("x1", p, k)])

        def tailB(i):
            p = i % 2
            x1 = x1b[p]
            t0 = i * T
            for t_ in range(2):
                for half in range(2):
                    pk = bank.next()
                    for kk in range(4):
                        k = half * 4 + kk
                        tr(PS[:, pk, kk * 128:(kk + 1) * 128], x1[:, k, t_ * 128:(t_ + 1) * 128], ident[:],
                           [("x1", p, k)], [("ps", pk)])
                    cp(("act", "dve")[half], ostg[t_][:, half * 512:(half + 1) * 512], PS[:, pk, :],
                       [("ps", pk)], [("ostg", t_)])
                dma("pool", out_d[t0 + t_ * 128:t0 + (t_ + 1) * 128, :], ostg[t_][:], [("ostg", t_)], [], ("ostg", t_))

        head(0)
        for i in range(NT):
            for f in range(32):
                up_group(i, f)
                if i > 0 and f == 7:
                    tailA(i - 1)
                if i > 0 and f == 15:
                    tailB(i - 1)
            for c in range(8):
                down_group(i, c)
                if c == 3 and i + 1 < NT:
                    head(i + 1)
        tailA(NT - 1)
        tailB(NT - 1)
        out_keys = [("ostg", 0), ("ostg", 1)]

    P.emit(final_wait_keys=out_keys)
    return nc


def _consts(S):
    pos = np.arange(S, dtype=np.float32)
    inv = (np.float32(500000.0) ** (-np.arange(0, 16, 2, dtype=np.float32) / np.float32(16))).astype(np.float32)
    ang = (pos[:, None] * inv[None, :]).astype(np.float32)
    cos = np.cos(ang).astype(np.float32)
    sin = np.sin(ang).astype(np.float32)
    cosT = np.ascontiguousarray(np.tile(cos.T, (16, 1)))
    sinT = np.ascontiguousarray(np.tile(sin.T, (16, 1)))
    invc = np.zeros((128, 4, 16), np.float32)
    for g, w in enumerate((2, 4, 8, 16)):
        invc[:, g, :] = 1.0 / np.minimum(np.arange(1, 17), w).astype(np.float32)
    return cosT, sinT, invc.reshape(128, 64), np.eye(128, dtype=np.float32)


def _col(v):
    v = np.asarray(v, np.float32).reshape(-1)
    return v.reshape(-1, 128).T


def make_in_maps(inputs, S, nb):
    f = lambda k: np.ascontiguousarray(np.asarray(inputs[k], dtype=np.float32))
    cosT, sinT, invc, ident = _consts(S)
    vecs = np.concatenate([
        _col(f("g_mix")[0]), _col(f("b_gate")[0, 0]), _col(f("b_gate")[0, 1]), _col(f("pool_scale")[0]),
        _col(f("g_subln")[0]), _col(f("g_mlp")[0]), _col(f("g_final"))], axis=1)
    assert vecs.shape == (128, NV)
    lams = np.concatenate([f("lambda_q1")[0], f("lambda_k1")[0], f("lambda_q2")[0], f("lambda_k2")[0]])
    lams = np.ascontiguousarray(np.tile(lams[None, :], (128, 1)))
    shared = {
        "w_in": f("w_in")[0], "pool_w": f("pool_w")[0], "w_pool_out": f("w_pool_out")[0],
        "w_attn_out": f("w_attn_out")[0], "w_o": f("w_o")[0], "w_up": f("w_up")[0], "w_down": f("w_down")[0],
        "vecs": np.ascontiguousarray(vecs), "lams": lams, "ident": ident, "cosT": cosT, "sinT": sinT, "invc": invc,
    }
    x = f("x")
    maps = []
    for b in range(nb):
        m = dict(shared)
        m["x"] = np.ascontiguousarray(x[b, :S])
        maps.append(m)
    return maps


def kernel(**inputs):
    x = np.asarray(inputs["x"])
    B, S, _ = x.shape
    nc = build_program(S)
    maps = make_in_maps(inputs, S, B)
    res = run_bass_kernel_spmd(nc, maps, core_ids=list(range(B)))
    return np.stack([np.asarray(r["out"], dtype=np.float32) for r in res.results], axis=0)
```

```python
import contextlib
import numpy as np
import concourse.bass as bass
import concourse.mybir as mybir
from concourse.bass_utils import run_bass_kernel_spmd

F32 = mybir.dt.float32
BF16 = mybir.dt.bfloat16
ALU = mybir.AluOpType
AF = mybir.ActivationFunctionType
AX = mybir.AxisListType

D = 1024
INW = 5632
DFF = 4096
O_U, O_Q, O_K, O_V, O_GA, O_GB = 0, 512, 1536, 2560, 3584, 4608
NV = 45
SB_BASE = 17408
SB_LIMIT = 229000

COMPUTE = ("pe", "act", "dve", "pool")
SAME_ENG_RAW = ("act", "dve", "pool")


class Op:
    __slots__ = ("eng", "idx", "fn", "deps", "dma_deps", "sig", "key", "cum", "is_dma")

    def __init__(self, eng, idx, fn):
        self.eng = eng
        self.idx = idx
        self.fn = fn
        self.deps = {}
        self.dma_deps = {}
        self.sig = None
        self.key = None
        self.cum = None
        self.is_dma = False


class Prog:
    def __init__(self, nc):
        self.nc = nc
        self.ops = {e: [] for e in ("pe", "act", "dve", "pool", "sp")}
        self.last_w = {}
        self.readers = {}
        self.dma_cum = {}
        self.dma_keys = []

    def _add_dep(self, op, src):
        if src is None or src is op:
            return
        if src.is_dma:
            op.dma_deps[src.key] = max(op.dma_deps.get(src.key, 0), self.dma_cum[src.key])
        else:
            op.deps[src.eng] = max(op.deps.get(src.eng, -1), src.idx)

    def op(self, eng, fn, reads=(), writes=(), dma_key=None):
        lst = self.ops[eng]
        o = Op(eng, len(lst), fn)
        if dma_key is not None:
            o.is_dma = True
            o.key = dma_key
        def same(src):
            return (not src.is_dma) and src.eng == eng and not o.is_dma

        for r in reads:
            src = self.last_w.get(r)
            if src is None:
                continue
            if same(src):
                if eng in SAME_ENG_RAW:
                    o.deps[eng] = max(o.deps.get(eng, -1), src.idx)
            else:
                self._add_dep(o, src)
        for w in writes:
            src = self.last_w.get(w)
            if src is not None:
                if same(src):
                    if eng in SAME_ENG_RAW:
                        o.deps[eng] = max(o.deps.get(eng, -1), src.idx)
                else:
                    self._add_dep(o, src)
            for rd in self.readers.get(w, ()):
                if same(rd):
                    if eng in SAME_ENG_RAW:
                        o.deps[eng] = max(o.deps.get(eng, -1), rd.idx)
                else:
                    self._add_dep(o, rd)
        if o.is_dma:
            if dma_key not in self.dma_cum:
                self.dma_cum[dma_key] = 0
                self.dma_keys.append(dma_key)
            self.dma_cum[dma_key] += 16
            o.cum = self.dma_cum[dma_key]
        for w in writes:
            self.last_w[w] = o
            self.readers[w] = []
        for r in reads:
            self.readers.setdefault(r, []).append(o)
        lst.append(o)
        return o

    def barrier(self):
        for e in self.ops:
            o = Op(e, len(self.ops[e]), None)
            for e2 in COMPUTE:
                if e2 == e:
                    continue
                j = len(self.ops[e2]) - 1
                while j >= 0 and (self.ops[e2][j].is_dma or self.ops[e2][j].fn is None):
                    j -= 1
                if j >= 0:
                    o.deps[e2] = j
            for k, c in self.dma_cum.items():
                o.dma_deps[k] = c
            self.ops[e].append(o)
        self.last_w = {}
        self.readers = {}

    def emit(self, final_wait_keys=()):
        nc = self.nc
        need = {e: set() for e in self.ops}
        for e, lst in self.ops.items():
            for o in lst:
                for e2, idx in o.deps.items():
                    need[e2].add(idx)
        sigmap = {}
        for e, lst in self.ops.items():
            n = 0
            arr = []
            for o in lst:
                if (not o.is_dma) and o.fn is not None and o.idx in need[e]:
                    n += 1
                    o.sig = n
                arr.append(n)
            sigmap[e] = arr
        _DBG["stats"] = ({e: (sigmap[e][-1] if sigmap[e] else 0) for e in sigmap}, {e: len(self.ops[e]) for e in self.ops}, max(self.dma_cum.values()), len(self.dma_keys))
        with contextlib.ExitStack() as st:
            sems = {e: st.enter_context(nc.semaphore("s_" + e)) for e in COMPUTE}
            dsem = {}
            for i, k in enumerate(self.dma_keys):
                dsem[k] = st.enter_context(nc.semaphore("d%d" % i))
            block = st.enter_context(nc.Block())
            engobj = {"pe": block.tensor, "act": block.scalar, "dve": block.vector,
                      "pool": block.gpsimd, "sp": block.sync}
            for e in ("sp", "pool", "act", "dve", "pe"):
                lst = self.ops[e]

                def body(eng, e=e, lst=lst):
                    known = {}
                    kdma = {}
                    for o in lst:
                        for e2, idx in o.deps.items():
                            v = sigmap[e2][idx]
                            if known.get(e2, 0) < v:
                                eng.wait_ge(sems[e2], v)
                                known[e2] = v
                        for k, c in o.dma_deps.items():
                            if kdma.get(k, 0) < c:
                                eng.wait_ge(dsem[k], c)
                                kdma[k] = c
                        if o.fn is None:
                            continue
                        ins = o.fn(eng)
                        if o.is_dma:
                            ins.then_inc(dsem[o.key], 16)
                        elif o.sig is not None:
                            ins.then_inc(sems[e], 1)
                    if e == "sp":
                        for k in final_wait_keys:
                            eng.wait_ge(dsem[k], self.dma_cum[k])

                engobj[e](body)


_UID = [0]
_DBG = {}


class Arena:
    def __init__(self, nc, base):
        self.nc = nc
        self.cur = base

    def alloc(self, shape, dtype):
        n = 1
        for s in shape[1:]:
            n *= s
        nbytes = n * (4 if dtype == F32 else 2)
        off = self.cur
        self.cur += (nbytes + 63) // 64 * 64
        assert self.cur <= SB_LIMIT, ("SBUF overflow", self.cur)
        _UID[0] += 1
        return self.nc.alloc_sbuf_tensor_at("t%d" % _UID[0], list(shape), dtype, offset=off)


class Rot:
    def __init__(self, n):
        self.n = n
        self.i = -1

    def next(self):
        self.i = (self.i + 1) % self.n
        return self.i


def build_program(S, dbg=False, phases="AB12"):
    nc = bass.Bass("TRN2", target_bir_lowering=False)
    P = Prog(nc)
    skind = "ExternalOutput" if dbg else "Internal"

    def din(name, shape, dt=F32):
        return nc.dram_tensor(name, list(shape), dt, kind="ExternalInput").ap()

    x_d = din("x", [S, D])
    w_in_d = din("w_in", [D, INW])
    pool_w_d = din("pool_w", [4, 128, 128])
    w_po_d = din("w_pool_out", [512, D])
    w_ao_d = din("w_attn_out", [D, D])
    w_o_d = din("w_o", [D, D])
    w_up_d = din("w_up", [D, DFF])
    w_dn_d = din("w_down", [DFF, D])
    vecs_d = din("vecs", [128, NV])
    lams_d = din("lams", [128, 256])
    ident_d = din("ident", [128, 128])
    cos_d = din("cosT", [128, S])
    sin_d = din("sinT", [128, S])
    invc_d = din("invc", [128, 64])
    out_d = nc.dram_tensor("out", [S, D], F32, kind="ExternalOutput").ap()

    QT_d = nc.dram_tensor("QT_s", [D, S], BF16, kind=skind).ap()
    KT_d = nc.dram_tensor("KT_s", [D, S], BF16, kind=skind).ap()
    V_d = nc.dram_tensor("V_s", [S, D], BF16, kind=skind).ap()
    mA_d = nc.dram_tensor("mA_s", [D, S], F32, kind=skind).ap()
    gB_d = nc.dram_tensor("gB_s", [D, S], F32, kind=skind).ap()
    aT_d = nc.dram_tensor("attT_s", [D, S], BF16, kind=skind).ap()

    def res_view(i, T):
        return out_d[i * T:(i + 1) * T, :].rearrange("a (b t) -> (a b) t", t=T)

    PSh = nc.alloc_psum_tensor("PS", [128, 8, 512], F32)
    PS = PSh
    PSb = PSh.bitcast(BF16)
    bank = Rot(8)

    def dma(q, out, in_, reads, writes, key):
        P.op(q, lambda e: e.dma_start(out=out, in_=in_), reads=reads, writes=writes, dma_key=key)

    def mm(out, lhsT, rhs, start, stop, reads, writes, skip=False):
        P.op("pe", lambda e: e.matmul(out, lhsT=lhsT, rhs=rhs, start=start, stop=stop, skip_group_check=skip),
             reads=reads, writes=writes)

    def tr(out, in_, idn, reads, writes):
        P.op("pe", lambda e: e.transpose(out=out, in_=in_, identity=idn), reads=reads, writes=writes)

    def act(out, in_, func, reads, writes, bias=None, scale=1.0):
        if bias is None:
            P.op("act", lambda e: e.activation(out=out, in_=in_, func=func, scale=scale), reads=reads, writes=writes)
        else:
            P.op("act", lambda e: e.activation(out=out, in_=in_, func=func, bias=bias, scale=scale),
                 reads=reads, writes=writes)

    def cp(eng, out, in_, reads, writes):
        if eng == "act":
            P.op("act", lambda e: e.copy(out=out, in_=in_), reads=reads, writes=writes)
        else:
            P.op(eng, lambda e: e.tensor_copy(out=out, in_=in_), reads=reads, writes=writes)

    def tt(eng, out, in0, in1, op, reads, writes):
        P.op(eng, lambda e: e.tensor_tensor(out=out, in0=in0, in1=in1, op=op), reads=reads, writes=writes)

    def ts(eng, out, in0, s1, s2, op0, op1, reads, writes):
        if s2 is None:
            P.op(eng, lambda e: e.tensor_scalar(out=out, in0=in0, scalar1=s1, scalar2=None, op0=op0),
                 reads=reads, writes=writes)
        else:
            P.op(eng, lambda e: e.tensor_scalar(out=out, in0=in0, scalar1=s1, scalar2=s2, op0=op0, op1=op1),
                 reads=reads, writes=writes)

    def stt(eng, out, in0, scalar, in1, op0, op1, reads, writes):
        P.op(eng, lambda e: e.scalar_tensor_tensor(out=out, in0=in0, scalar=scalar, in1=in1, op0=op0, op1=op1),
             reads=reads, writes=writes)

    def memset(eng, ap, val, writes):
        P.op(eng, lambda e: e.memset(ap, val), writes=writes)

    def recip(out, in_, reads, writes):
        P.op("dve", lambda e: e.reciprocal(out=out, in_=in_), reads=reads, writes=writes)

    def rsum(out, in_, reads, writes):
        P.op("dve", lambda e: e.tensor_reduce(out=out, in_=in_, axis=AX.X, op=ALU.add), reads=reads, writes=writes)

    A0 = Arena(nc, SB_BASE)
    vec = A0.alloc([128, NV], F32)
    ident = A0.alloc([128, 128], F32)
    identb = A0.alloc([128, 128], BF16)
    onesb = A0.alloc([128, 128], BF16)
    lam_t = A0.alloc([128, 256], F32)
    lam_p = A0.alloc([128, 64], F32)
    lam_s = A0.alloc([128, 4], F32)
    neglam = A0.alloc([128, 1], F32)
    eps6 = A0.alloc([128, 1], F32)
    eps5 = A0.alloc([128, 1], F32)
    invc = A0.alloc([128, 64], F32)
    PH_BASE = A0.cur

    dma("sp", vec[:], vecs_d, [], ["vec"], "c_vec")
    dma("sp", ident[:], ident_d, [], ["ident"], "c_ident")
    dma("sp", lam_t[:], lams_d, [], ["lam_t"], "c_lam")
    dma("sp", invc[:], invc_d, [], ["invc"], "c_invc")
    memset("dve", onesb[:], 1.0 / 1024.0, ["onesb"])
    memset("dve", eps6[:], 1e-6, ["eps6"])
    memset("dve", eps5[:], 1e-5, ["eps5"])
    cp("dve", identb[:], ident[:], ["ident"], ["identb"])
    for k in range(2):
        tt("dve", lam_p[:], lam_t[:, 128 * k:128 * k + 64], lam_t[:, 128 * k + 64:128 * k + 128], ALU.mult,
           ["lam_t"], ["lam_p"])
        rsum(lam_s[:, k:k + 1], lam_p[:], ["lam_p"], ["lam_s"])
    act(lam_s[:, 2:4], lam_s[:, 0:2], AF.Exp, ["lam_s"], ["lam_e"])
    tt("dve", neglam[:], lam_s[:, 3:4], lam_s[:, 2:3], ALU.subtract, ["lam_e"], ["neglam"])
    ts("dve", neglam[:], neglam[:], -0.2, None, ALU.add, None, ["neglam"], ["neglam"])

    cast_eng = Rot(3)
    CAST_ENGS = ("dve", "pool", "act")

    def load_cast(stage, srot, dst, src, ncols, perm=False, src_view=None):
        sl = srot.next()
        st_ap = stage[sl][:, 0:ncols] if src_view is None else src_view(stage[sl])
        dma("sp", st_ap, src, [], [("stage", sl)], ("stage", sl))
        if perm:
            eng = ("dve", "pool")[cast_eng.next() % 2]
            iv = stage[sl][:, 0:1024].rearrange("p (hc j dd) -> p j hc dd", hc=16, j=8, dd=8)
            ov = dst.rearrange("p (j hc dd) -> p j hc dd", j=8, hc=16, dd=8)
            _UID[0] += 1
            cp(eng, ov, iv, [("stage", sl)], [("W", _UID[0])])
        else:
            eng = CAST_ENGS[cast_eng.next()]
            _UID[0] += 1
            cp(eng, dst, stage[sl][:, 0:ncols], [("stage", sl)], [("W", _UID[0])])

    if "A" in phases:
        T = 256
        NT = S // T
        L = 16 + T
        A = Arena(nc, PH_BASE)
        Wb = A.alloc([128, 8, INW], BF16)
        Wpo = A.alloc([128, 4, D], BF16)
        Wpw = A.alloc([128, 4, 128], BF16)
        mark = A.cur
        stage = [A.alloc([128, 1024], F32) for _ in range(6)]
        srot = Rot(6)
        for dc in range(8):
            rows = w_in_d[dc * 128:(dc + 1) * 128, :]
            for (o, n, perm) in ((O_U, 512, False), (O_Q, 1024, True), (O_K, 1024, True),
                                 (O_V, 1024, False), (O_GA, 1024, False), (O_GB, 1024, False)):
                load_cast(stage, srot, Wb[:, dc, o:o + n], rows[:, o:o + n], n, perm)
        for g in range(4):
            load_cast(stage, srot, Wpo[:, g, :], w_po_d[g * 128:(g + 1) * 128, :], 1024)
        load_cast(stage, srot, Wpw[:].rearrange("p g q -> p (g q)"), pool_w_d.rearrange("g p q -> p g q"), 512,
                  src_view=lambda s: s[:, 0:512].rearrange("p (g q) -> p g q", g=4))
        P.barrier()
        A.cur = mark
        xin = [A.alloc([128, 2, D], F32) for _ in range(2)]
        xT2 = [A.alloc([128, 8, T], F32) for _ in range(2)]
        sq2 = [A.alloc([128, 8, T], BF16) for _ in range(2)]
        R1 = A.alloc([128, 8, T], BF16)
        rs2 = [A.alloc([128, T], F32) for _ in range(2)]
        hT2 = [A.alloc([128, 8, T], BF16) for _ in range(2)]
        u = A.alloc([128, 4, L], F32)
        tA = A.alloc([128, L], F32)
        tB = A.alloc([128, L], F32)
        t16 = A.alloc([128, 16], F32)
        gA = [A.alloc([128, T], F32) for _ in range(3)]
        stgf = [A.alloc([128, T], F32) for _ in range(8)]
        cs = [A.alloc([128, 2, T], F32) for _ in range(2)]
        rt = [A.alloc([128, T], F32) for _ in range(4)]
        qs = [A.alloc([128, 8, T], BF16) for _ in range(2)]
        vst = [A.alloc([128, D], BF16) for _ in range(2)]
        frot = Rot(8)
        memset("pool", u[:], 0.0, [("u", g) for g in range(4)])

        def load_x(i):
            b = i % 2
            dma("sp", xin[b][:], x_d[i * T:(i + 1) * T, :].rearrange("(t p) d -> p t d", p=128),
                [], [("xin", b)], ("xin", b))
            dma("sp", cs[b][:, 0, :], cos_d[:, i * T:(i + 1) * T], [], [("cs", b)], ("cs", b))
            dma("sp", cs[b][:, 1, :], sin_d[:, i * T:(i + 1) * T], [], [("cs", b)], ("cs", b))

        cur = [0]

        def proj(col0):
            pk = bank.next()
            hb = cur[0] % 2
            for dc in range(8):
                mm(PS[:, pk, 0:T], Wb[:, dc, col0:col0 + 128], hT2[hb][:, dc, :], dc == 0, dc == 7,
                   [("hT", hb, dc)], [("ps", pk)])
            return pk

        def prologue_a(i):
            xb = i % 2
            xT, sq = xT2[xb], sq2[xb]
            for dc in range(8):
                pk = bank.next()
                for t_ in range(2):
                    tr(PS[:, pk, t_ * 128:(t_ + 1) * 128], xin[xb][:, t_, dc * 128:(dc + 1) * 128], ident[:],
                       [("xin", xb)], [("ps", pk)])
                cp("dve", xT[:, dc, :], PS[:, pk, 0:T], [("ps", pk)], [("xT", xb, dc)])
                act(sq[:, dc, :], xT[:, dc, :], AF.Square, [("xT", xb, dc)], [("sq", xb, dc)])
            dma("pool", res_view(i, T).rearrange("(c p) t -> p c t", p=128), xT[:],
                [("xT", xb, dc) for dc in range(8)], [], ("xTst", xb))

        def prologue_b(i):
            xb = i % 2
            xT, sq, rs, hT = xT2[xb], sq2[xb], rs2[xb], hT2[xb]
            pk = bank.next()
            for dc in range(8):
                mm(PS[:, pk, 0:T], onesb[:], sq[:, dc, :], dc == 0, dc == 7, [("sq", xb, dc)], [("ps", pk)])
            act(rs[:], PS[:, pk, 0:T], AF.Sqrt, [("ps", pk)], [("rs", xb)], bias=eps6[:, 0:1])
            recip(rs[:], rs[:], [("rs", xb)], [("rs", xb)])
            for dc in range(8):
                stt("dve", hT[:, dc, :], xT[:, dc, :], vec[:, dc:dc + 1], rs[:], ALU.mult, ALU.mult,
                    [("xT", xb, dc), ("rs", xb)], [("hT", xb, dc)])

        load_x(0)
        if NT > 1:
            load_x(1)
        prologue_a(0)
        prologue_b(0)
        for i in range(NT if _DBG.get("stop") != "prep" else 0):
            t0 = i * T
            xb = i % 2
            cur[0] = i
            hT = hT2[xb]
            for g in range(4):
                pk = proj(O_U + g * 128)
                cp("act", u[:, g, 16:L], PS[:, pk, 0:T], [("ps", pk)], [("u", g)])
            if _DBG.get("stop") == "u":
                continue
            for wi, (o_, dst) in enumerate(((O_Q, QT_d), (O_K, KT_d))):
                p0 = proj(o_)
                p1 = proj(o_ + 128)
                cosb = cs[xb][:, 0, :]
                sinb = cs[xb][:, 1, :]
                tt("dve", rt[0][:], PS[:, p0, 0:T], cosb, ALU.mult, [("ps", p0), ("cs", xb)], [("rt", 0)])
                tt("dve", rt[1][:], PS[:, p1, 0:T], sinb, ALU.mult, [("ps", p1), ("cs", xb)], [("rt", 1)])
                tt("dve", qs[wi][:, 0, :], rt[0][:], rt[1][:], ALU.subtract, [("rt", 0), ("rt", 1)], [("qs", wi)])
                tt("dve", rt[2][:], PS[:, p1, 0:T], cosb, ALU.mult, [("ps", p1), ("cs", xb)], [("rt", 2)])
                tt("dve", rt[3][:], PS[:, p0, 0:T], sinb, ALU.mult, [("ps", p0), ("cs", xb)], [("rt", 3)])
                tt("dve", qs[wi][:, 1, :], rt[2][:], rt[3][:], ALU.add, [("rt", 2), ("rt", 3)], [("qs", wi)])
                for j in range(2, 8):
                    pk = proj(o_ + j * 128)
                    cp(("act", "dve")[j % 2], qs[wi][:, j, :], PS[:, pk, 0:T], [("ps", pk)], [("qs", wi)])
                dma("pool", dst.rearrange("(j p) t -> p j t", p=128)[:, :, t0:t0 + T], qs[wi][:],
                    [("qs", wi)], [], ("qs", wi))
            if _DBG.get("stop") == "qk":
                continue
            for g in range(4):
                U = u[:, g, :]
                tt("pool", tA[:, 1:L], U[:, 1:L], U[:, 0:L - 1], ALU.add, [("u", g)], ["tA"])
                win = tA
                if g >= 1:
                    tt("pool", tB[:, 3:L], tA[:, 3:L], tA[:, 1:L - 2], ALU.add, ["tA"], ["tB"])
                    win = tB
                if g >= 2:
                    tt("pool", tA[:, 7:L], tB[:, 7:L], tB[:, 3:L - 4], ALU.add, ["tB"], ["tA"])
                    win = tA
                if g >= 3:
                    tt("pool", tB[:, 15:L], tA[:, 15:L], tA[:, 7:L - 8], ALU.add, ["tA"], ["tB"])
                    win = tB
                wtok = "tA" if win is tA else "tB"
                stt("dve", R1[:, g, :], win[:, 16:L], 1.0 / (2 ** (g + 1)), U[:, 16:L], ALU.mult, ALU.subtract,
                    [wtok, ("u", g)], [("R1", g)])
                if i == 0:
                    tt("pool", t16[:], win[:, 16:32], invc[:, g * 16:(g + 1) * 16], ALU.mult, [wtok, "invc"], ["t16"])
                    tt("pool", R1[:, g, 0:16], t16[:], U[:, 16:32], ALU.subtract, ["t16", ("u", g)], [("R1", g)])
            cp("pool", u[:, :, 0:16], u[:, :, T:T + 16], [("u", g) for g in range(4)], [("u", g) for g in range(4)])
            if _DBG.get("stop") == "mix":
                continue
            for t_ in range(2):
                for half in range(2):
                    pk = bank.next()
                    for dc in range(8):
                        mm(PS[:, pk, :], hT[:, dc, t_ * 128:(t_ + 1) * 128],
                           Wb[:, dc, O_V + half * 512:O_V + (half + 1) * 512], dc == 0, dc == 7,
                           [("hT", xb, dc)], [("ps", pk)])
                    cp(("act", "dve")[half], vst[t_][:, half * 512:(half + 1) * 512], PS[:, pk, :],
                       [("ps", pk)], [("vst", t_)])
                dma("pool", V_d[t0 + t_ * 128:t0 + (t_ + 1) * 128, :], vst[t_][:], [("vst", t_)], [], ("vst", t_))
            if _DBG.get("stop") == "ga":
                continue
            for c in range(8):
                pk = proj(O_GB + c * 128)
                r = frot.next()
                act(stgf[r][:], PS[:, pk, 0:T], AF.Sigmoid, [("ps", pk)], [("stgf", r)], bias=vec[:, 16 + c:17 + c])
                dma("pool", gB_d[c * 128:(c + 1) * 128, t0:t0 + T], stgf[r][:], [("stgf", r)], [], ("stgf", r))
            if _DBG.get("stop") == "pool":
                continue
            for g in range(4):
                pk = bank.next()
                mm(PS[:, pk, 0:T], Wpw[:, g, :], R1[:, g, :], True, True, [("R1", g)], [("ps", pk)])
                ts("dve", R1[:, 4 + g, :], PS[:, pk, 0:T], vec[:, 24 + g:25 + g], None, ALU.mult, None,
                   [("ps", pk)], [("R1", 4 + g)])
            if i + 2 < NT:
                load_x(i + 2)
            if i + 1 < NT:
                prologue_a(i + 1)
            for c in range(8):
                if c == 3 and i + 1 < NT:
                    prologue_b(i + 1)
                pkg = proj(O_GA + c * 128)
                act(gA[c % 3][:], PS[:, pkg, 0:T], AF.Sigmoid, [("ps", pkg)], [("gA", c % 3)], bias=vec[:, 8 + c:9 + c])
                pky = bank.next()
                for g in range(4):
                    mm(PS[:, pky, 0:T], Wpo[:, g, c * 128:(c + 1) * 128], R1[:, 4 + g, :], g == 0, g == 3,
                       [("R1", 4 + g)], [("ps", pky)])
                r = frot.next()
                tt("dve", stgf[r][:], PS[:, pky, 0:T], gA[c % 3][:], ALU.mult, [("ps", pky), ("gA", c % 3)], [("stgf", r)])
                dma("pool", mA_d[c * 128:(c + 1) * 128, t0:t0 + T], stgf[r][:], [("stgf", r)], [], ("stgf", r))
        P.barrier()

    if "B" in phases:
        NQB = S // 512
        NKB = S // 128
        B = Arena(nc, PH_BASE)
        qt = [B.alloc([128, S], BF16) for _ in range(2)]
        ktp = [[B.alloc([128, S], BF16) for _ in range(2)] for _ in range(2)]
        vt = [B.alloc([128, NKB, 129], BF16) for _ in range(2)]
        E = [B.alloc([128, 2, 512], BF16) for _ in range(4)]
        att = B.alloc([128, 4, 128], F32)
        sq4 = B.alloc([128, 4, 128], F32)
        ss = B.alloc([128, 4], F32)
        rs4 = B.alloc([128, 4], F32)
        rl = B.alloc([128, 9], F32)
        attb = B.alloc([128, 4, 128], BF16)
        ast = [B.alloc([128, 512], BF16) for _ in range(2)]
        for b in range(2):
            memset("pool", vt[b][:, :, 128:129], 1.0, [("vt", b)])
            for c in range(2):
                memset(("dve", "pool")[c], ktp[b][c][:], 0.0, [("kt", b)])

        def acc(a):
            return PS[:, 4 + a // 3, (a % 3) * 129:(a % 3) * 129 + 129]

        def load_head(h):
            hb = h % 2
            for c in range(2):
                for j in range(8):
                    r0 = j * 128 + (2 * h + c) * 8
                    p0 = c * 64 + j * 8
                    dma("sp", qt[hb][p0:p0 + 8, :], QT_d[r0:r0 + 8, :], [], [("qt", hb)], ("qt", hb))
                    dma("sp", ktp[hb][c][p0:p0 + 8, :], KT_d[r0:r0 + 8, :], [], [("kt", hb)], ("kt", hb))
            vsrc = V_d.rearrange("(k p) f -> p k f", p=128)
            nch = max(1, NKB // 16)
            for q4 in range(nch):
                k0 = q4 * (NKB // nch)
                k1 = (q4 + 1) * (NKB // nch)
                dma("sp", vt[hb][:, k0:k1, 0:128], vsrc[:, k0:k1, h * 128:(h + 1) * 128], [], [("vt", hb)], ("vt", hb))

        steps = [(h, i, j) for h in range(8) for i in range(NQB) for j in range(4 * i + 4)]
        deferred = []
        arot = Rot(2)

        def QK(n):
            h, i, j = steps[n]
            sb, hb = n % 2, h % 2
            jj = j - 4 * i
            n0 = 128 * jj if jj > 0 else 0
            for c in range(2):
                mm(PS[:, 2 * sb + c, n0:512], ktp[hb][c][:, j * 128:(j + 1) * 128],
                   qt[hb][:, i * 512 + n0:(i + 1) * 512], True, True, [("kt", hb), ("qt", hb)], [("S", sb)])

        def EXP(n):
            h, i, j = steps[n]
            sb, eb = n % 2, n % 4
            jj = j - 4 * i
            n0 = 128 * jj if jj > 0 else 0
            act(E[eb][:, :, n0:512], PS[:, 2 * sb:2 * sb + 2, n0:512], AF.Exp, [("S", sb)], [("E", eb)], scale=0.125)
            if jj >= 0:
                memset("pool", E[eb][64:128, :, n0:n0 + 64], 0.0, [("E", eb)])

        def AV(n):
            h, i, j = steps[n]
            eb, hb = n % 4, h % 2
            jj = j - 4 * i
            for t_ in range(max(jj, 0), 4):
                for c in range(2):
                    a = t_ * 2 + c
                    mm(acc(a), E[eb][:, c, t_ * 128:(t_ + 1) * 128], vt[hb][:, j, :],
                       (j == 0 and a % 3 == 0), (j == 4 * i + t_), [("E", eb), ("vt", hb)], [("acc", a // 3)], skip=True)

        def epilogue(h, i):
            for b in range(3):
                ncol = 3 if b < 2 else 2
                recip(rl[:, 3 * b:3 * b + ncol], PS[:, 4 + b, 128:128 + 129 * (ncol - 1) + 1:129], [("acc", b)], ["rl"])
            ts("dve", rl[:, 1:8:2], rl[:, 1:8:2], neglam[:, 0:1], None, ALU.mult, None, ["rl", "neglam"], ["rl"])
            for t_ in range(4):
                ts("dve", att[:, t_, :], acc(2 * t_)[:, 0:128], rl[:, 2 * t_:2 * t_ + 1], None, ALU.mult, None,
                   [("acc", (2 * t_) // 3), "rl"], [("att", t_)])
                stt("dve", att[:, t_, :], acc(2 * t_ + 1)[:, 0:128], rl[:, 2 * t_ + 1:2 * t_ + 2], att[:, t_, :],
                    ALU.mult, ALU.add, [("acc", (2 * t_ + 1) // 3), "rl", ("att", t_)], [("att", t_)])
            atoks = [("att", t_) for t_ in range(4)]
            tt("pool", sq4[:], att[:], att[:], ALU.mult, atoks, ["sq4"])
            rsum(ss[:], sq4[:], ["sq4"], ["ss"])
            def late():
                act(rs4[:], ss[:], AF.Ln, ["ss"], ["rs4"], bias=eps5[:, 0:1], scale=1.0 / 128.0)
                act(rs4[:], rs4[:], AF.Exp, ["rs4"], ["rs4"], scale=-0.5)
                for t_ in range(4):
                    ts("dve", attb[:, t_, :], att[:, t_, :], rs4[:, t_:t_ + 1], None, ALU.mult, None,
                       [("att", t_), "rs4"], [("attb", t_)])
                deferred.append([2, late2])

            def late2():
                for t_ in range(4):
                    tr(PSb[:, 7, t_ * 128:(t_ + 1) * 128], attb[:, t_, :], identb[:], [("attb", t_)], ["ptr"])
                sa = arot.next()
                ts("dve", ast[sa][:], PSb[:, 7, 0:512], vec[:, 28:29], 0.8, ALU.mult, ALU.mult, ["ptr"], [("ast", sa)])
                dma("pool", aT_d[h * 128:(h + 1) * 128, i * 512:(i + 1) * 512], ast[sa][:], [("ast", sa)], [], ("ast", sa))
            return late

        load_head(0)
        NS = len(steps)
        for n in range(NS + 2):
            if n < NS:
                QK(n)
                EXP(n)
            if n >= 2:
                h, i, j = steps[n - 2]
                AV(n - 2)
                if j == 4 * i + 3:
                    deferred.append([3, epilogue(h, i)])
            if n < NS:
                h, i, j = steps[n]
                if i == 0 and j == 0 and h + 1 < 8 and n >= 0:
                    pending_head = h + 1
            if n >= 1 and n - 1 < NS:
                h, i, j = steps[n - 1]
                if i == 0 and j == 0 and h + 1 < 8:
                    load_head(h + 1)
            for dfr in list(deferred):
                if dfr[0] == 0 or n == NS + 1:
                    dfr[1]()
                    deferred.remove(dfr)
                else:
                    dfr[0] -= 1
        while deferred:
            deferred.pop(0)[1]()
        P.barrier()

    if "1" in phases:
        T = 256
        NT = S // T
        C = Arena(nc, PH_BASE)
        Wao = C.alloc([128, 8, D], BF16)
        Wo = C.alloc([128, 8, D], BF16)
        mark = C.cur
        stage = [C.alloc([128, 1024], F32) for _ in range(6)]
        srot = Rot(6)
        for k in range(8):
            load_cast(stage, srot, Wao[:, k, :], w_ao_d[k * 128:(k + 1) * 128, :], 1024)
        for k in range(8):
            load_cast(stage, srot, Wo[:, k, :], w_o_d[k * 128:(k + 1) * 128, :], 1024)
        P.barrier()
        C.cur = mark
        at = [C.alloc([128, 8, T], BF16) for _ in range(2)]
        mAb = [C.alloc([128, T], F32) for _ in range(3)]
        gBb = [C.alloc([128, T], F32) for _ in range(3)]
        xTb = [C.alloc([128, T], F32) for _ in range(3)]
        tmp = [C.alloc([128, T], F32) for _ in range(2)]
        mg = C.alloc([128, 8, T], BF16)
        x1s = [C.alloc([128, T], F32) for _ in range(3)]
        r3 = Rot(3)
        rx = Rot(3)
        ro = Rot(3)

        def load_at(i):
            b = i % 2
            dma("sp", at[b][:], aT_d.rearrange("(k p) t -> p k t", p=128)[:, :, i * T:(i + 1) * T],
                [], [("at", b)], ("at", b))

        def ya_group(i, c):
            ab = i % 2
            t0 = i * T
            r = r3.next()
            dma("sp", mAb[r][:], mA_d[c * 128:(c + 1) * 128, t0:t0 + T], [], [("mAb", r)], ("mAb", r))
            dma("sp", gBb[r][:], gB_d[c * 128:(c + 1) * 128, t0:t0 + T], [], [("gBb", r)], ("gBb", r))
            pk = bank.next()
            for k in range(8):
                mm(PS[:, pk, 0:T], Wao[:, k, c * 128:(c + 1) * 128], at[ab][:, k, :], k == 0, k == 7,
                   [("at", ab)], [("ps", pk)])
            tt("dve", tmp[c % 2][:], PS[:, pk, 0:T], gBb[r][:], ALU.mult, [("ps", pk), ("gBb", r)], [("tmp", c % 2)])
            tt("pool", mg2[ab][:, c, :], tmp[c % 2][:], mAb[r][:], ALU.add, [("tmp", c % 2), ("mAb", r)], [("mg", ab, c)])

        def wo_group(i, c2):
            ab = i % 2
            r = rx.next()
            dma("sp", xTb[r][:], res_view(i, T)[c2 * 128:(c2 + 1) * 128, :], [], [("xTb", r)], ("xTb", r))
            pk = bank.next()
            for c in range(8):
                mm(PS[:, pk, 0:T], Wo[:, c, c2 * 128:(c2 + 1) * 128], mg2[ab][:, c, :], c == 0, c == 7,
                   [("mg", ab, c)], [("ps", pk)])
            o = ro.next()
            tt("dve", x1s[o][:], PS[:, pk, 0:T], xTb[r][:], ALU.add, [("ps", pk), ("xTb", r)], [("x1s", o)])
            dma("pool", res_view(i, T)[c2 * 128:(c2 + 1) * 128, :], x1s[o][:], [("x1s", o)], [], ("x1s", o))

        mg2 = [mg, C.alloc([128, 8, T], BF16)]
        load_at(0)
        for i in range(NT + 1):
            if i + 1 < NT:
                load_at(i + 1)
            for c in range(8):
                if i < NT:
                    ya_group(i, c)
                if i >= 1:
                    wo_group(i - 1, c)
        P.barrier()

    out_keys = []
    if "2" in phases:
        T = 256
        NT = S // T
        C = Arena(nc, PH_BASE)
        Wup = C.alloc([128, 8, DFF], BF16)
        Wdn = C.alloc([128, 32, D], BF16)
        mark = C.cur
        stage = [C.alloc([128, 1024], F32) for _ in range(4)]
        srot = Rot(4)
        for k in range(8):
            for q4 in range(4):
                load_cast(stage, srot, Wup[:, k, q4 * 1024:(q4 + 1) * 1024],
                          w_up_d[k * 128:(k + 1) * 128, q4 * 1024:(q4 + 1) * 1024], 1024)
        for f in range(32):
            load_cast(stage, srot, Wdn[:, f, :], w_dn_d[f * 128:(f + 1) * 128, :], 1024)
        P.barrier()
        C.cur = mark
        x1b = [C.alloc([128, 8, T], F32) for _ in range(2)]
        h2b = [C.alloc([128, 8, T], BF16) for _ in range(2)]
        sqa = C.alloc([128, 8, T], BF16)
        sqb = C.alloc([128, 8, T], BF16)
        rsa = [C.alloc([128, T], F32) for _ in range(2)]
        rsb = C.alloc([128, T], F32)
        aT = C.alloc([128, 32, T], BF16)
        rtmp = [C.alloc([128, T], F32) for _ in range(3)]
        ostg = [C.alloc([128, D], F32) for _ in range(2)]
        rr = Rot(3)

        def head(i):
            p = i % 2
            x1 = x1b[p]
            dma("sp", x1[:], res_view(i, T).rearrange("(k p) t -> p k t", p=128),
                [], [("x1", p, k) for k in range(8)], ("x1", p))
            for k in range(8):
                act(sqa[:, k, :], x1[:, k, :], AF.Square, [("x1", p, k)], [("sqa", k)])
            pk = bank.next()
            for k in range(8):
                mm(PS[:, pk, 0:T], onesb[:], sqa[:, k, :], k == 0, k == 7, [("sqa", k)], [("ps", pk)])
            act(rsa[p][:], PS[:, pk, 0:T], AF.Sqrt, [("ps", pk)], [("rsa", p)], bias=eps6[:, 0:1])
            recip(rsa[p][:], rsa[p][:], [("rsa", p)], [("rsa", p)])
            for k in range(8):
                stt("dve", h2b[p][:, k, :], x1[:, k, :], vec[:, 29 + k:30 + k], rsa[p][:], ALU.mult, ALU.mult,
                    [("x1", p, k), ("rsa", p)], [("h2", p, k)])

        def up_group(i, f):
            p = i % 2
            pk = bank.next()
            for k in range(8):
                mm(PS[:, pk, 0:T], Wup[:, k, f * 128:(f + 1) * 128], h2b[p][:, k, :], k == 0, k == 7,
                   [("h2", p, k)], [("ps", pk)])
            r = rr.next()
            act(rtmp[r][:], PS[:, pk, 0:T], AF.Relu, [("ps", pk)], [("rtmp", r)])
            tt("pool", aT[:, f, :], rtmp[r][:], rtmp[r][:], ALU.mult, [("rtmp", r)], [("aT", f)])

        def down_group(i, c):
            p = i % 2
            pk = bank.next()
            for f in range(32):
                mm(PS[:, pk, 0:T], Wdn[:, f, c * 128:(c + 1) * 128], aT[:, f, :], f == 0, f == 31,
                   [("aT", f)], [("ps", pk)])
            tt("dve", x1b[p][:, c, :], PS[:, pk, 0:T], x1b[p][:, c, :], ALU.add, [("ps", pk), ("x1", p, c)], [("x1", p, c)])

        def tailA(i):
            p = i % 2
            x1 = x1b[p]
            for k in range(8):
                act(sqb[:, k, :], x1[:, k, :], AF.Square, [("x1", p, k)], [("sqb", k)])
            pk = bank.next()
            for k in range(8):
                mm(PS[:, pk, 0:T], onesb[:], sqb[:, k, :], k == 0, k == 7, [("sqb", k)], [("ps", pk)])
            act(rsb[:], PS[:, pk, 0:T], AF.Sqrt, [("ps", pk)], ["rsb"], bias=eps6[:, 0:1])
            recip(rsb[:], rsb[:], ["rsb"], ["rsb"])
            for k in range(8):
                stt("dve", x1[:, k, :], x1[:, k, :], vec[:, 37 + k:38 + k], rsb[:], ALU.mult, ALU.mult,
                    [("x1", p, k), "rsb"], [("x1", p, k)])

        def tailB(i):
            p = i % 2
            x1 = x1b[p]
            t0 = i * T
            for t_ in range(2):
                for half in range(2):
                    pk = bank.next()
                    for kk in range(4):
                        k = half * 4 + kk
                        tr(PS[:, pk, kk * 128:(kk + 1) * 128], x1[:, k, t_ * 128:(t_ + 1) * 128], ident[:],
                           [("x1", p, k)], [("ps", pk)])
                    cp(("act", "dve")[half], ostg[t_][:, half * 512:(half + 1) * 512], PS[:, pk, :],
                       [("ps", pk)], [("ostg", t_)])
                dma("pool", out_d[t0 + t_ * 128:t0 + (t_ + 1) * 128, :], ostg[t_][:], [("ostg", t_)], [], ("ostg", t_))

        head(0)
        for i in range(NT):
            for f in range(32):
                up_group(i, f)
                if i > 0 and f == 7:
                    tailA(i - 1)
                if i > 0 and f == 15:
                    tailB(i - 1)
            for c in range(8):
                down_group(i, c)
                if c == 3 and i + 1 < NT:
                    head(i + 1)
        tailA(NT - 1)
        tailB(NT - 1)
        out_keys = [("ostg", 0), ("ostg", 1)]

    P.emit(final_wait_keys=out_keys)
    return nc


def _consts(S):
    pos = np.arange(S, dtype=np.float32)
    inv = (np.float32(500000.0) ** (-np.arange(0, 16, 2, dtype=np.float32) / np.float32(16))).astype(np.float32)
    ang = (pos[:, None] * inv[None, :]).astype(np.float32)
    cos = np.cos(ang).astype(np.float32)
    sin = np.sin(ang).astype(np.float32)
    cosT = np.ascontiguousarray(np.tile(cos.T, (16, 1)))
    sinT = np.ascontiguousarray(np.tile(sin.T, (16, 1)))
    invc = np.zeros((128, 4, 16), np.float32)
    for g, w in enumerate((2, 4, 8, 16)):
        invc[:, g, :] = 1.0 / np.minimum(np.arange(1, 17), w).astype(np.float32)
    return cosT, sinT, invc.reshape(128, 64), np.eye(128, dtype=np.float32)


def _col(v):
    v = np.asarray(v, np.float32).reshape(-1)
    return v.reshape(-1, 128).T


def make_in_maps(inputs, S, nb):
    f = lambda k: np.ascontiguousarray(np.asarray(inputs[k], dtype=np.float32))
    cosT, sinT, invc, ident = _consts(S)
    vecs = np.concatenate([
        _col(f("g_mix")[0]), _col(f("b_gate")[0, 0]), _col(f("b_gate")[0, 1]), _col(f("pool_scale")[0]),
        _col(f("g_subln")[0]), _col(f("g_mlp")[0]), _col(f("g_final"))], axis=1)
    assert vecs.shape == (128, NV)
    lams = np.concatenate([f("lambda_q1")[0], f("lambda_k1")[0], f("lambda_q2")[0], f("lambda_k2")[0]])
    lams = np.ascontiguousarray(np.tile(lams[None, :], (128, 1)))
    shared = {
        "w_in": f("w_in")[0], "pool_w": f("pool_w")[0], "w_pool_out": f("w_pool_out")[0],
        "w_attn_out": f("w_attn_out")[0], "w_o": f("w_o")[0], "w_up": f("w_up")[0], "w_down": f("w_down")[0],
        "vecs": np.ascontiguousarray(vecs), "lams": lams, "ident": ident, "cosT": cosT, "sinT": sinT, "invc": invc,
    }
    x = f("x")
    maps = []
    for b in range(nb):
        m = dict(shared)
        m["x"] = np.ascontiguousarray(x[b, :S])
        maps.append(m)
    return maps


def kernel(**inputs):
    x = np.asarray(inputs["x"])
    B, S, _ = x.shape
    nc = build_program(S)
    maps = make_in_maps(inputs, S, B)
    res = run_bass_kernel_spmd(nc, maps, core_ids=list(range(B)))
    return np.stack([np.asarray(r["out"], dtype=np.float32) for r in res.results], axis=0)
```
